# Optimizing a Trainium2 kernel written in Bass

```python
import math
import jax, jax.numpy as jnp
from jax import lax
import numpy as np

D_MODEL = 1024
BATCH = 32
SEQ = 256
DEPTH = 1
DEC_BATCH = 4
DEC_SEQ = 2048
PAST_LEN = 512

GRID_W = 64
N_HEADS = 4
HEAD_DIM = 64
V_DIM = 2 * HEAD_DIM
ATTN_WIDTH = N_HEADS * V_DIM
QK_WIDTH = N_HEADS * 2 * HEAD_DIM
N_FOUR_GROUPS = 4
FOUR_GROUP = 128
FOUR_WIDTH = N_FOUR_GROUPS * FOUR_GROUP
PROJ_WIDTH = 2 * QK_WIDTH + ATTN_WIDTH + FOUR_WIDTH
MIX_WIDTH = ATTN_WIDTH + FOUR_WIDTH
D_FF = 4 * D_MODEL
ROPE_BASE = 10000.0
ROPE_AXIS_DIM = HEAD_DIM // 2
Q_BLOCK = 128
EPS = 1e-6

kernel_name = "hybrid_diffattn_fnet_prefix_dit_step"


def rms_norm(x, g):
    xf = x.astype(jnp.float32)
    y = xf * lax.rsqrt(jnp.mean(xf * xf, axis=-1, keepdims=True) + EPS)
    return (y * g.astype(jnp.float32)).astype(x.dtype)


def adaln(cvec, w_mod, b_mod):
    m = jax.nn.silu(cvec) @ w_mod + b_mod
    return jnp.split(m[:, None, :], 6, axis=-1)


def rope_tables(n):
    rows = n // GRID_W
    row = jnp.broadcast_to(jnp.arange(rows)[:, None], (rows, GRID_W)).reshape(n).astype(jnp.float32)
    col = jnp.broadcast_to(jnp.arange(GRID_W)[None, :], (rows, GRID_W)).reshape(n).astype(jnp.float32)
    inv = ROPE_BASE ** (-jnp.arange(0, ROPE_AXIS_DIM, 2, dtype=jnp.float32) / ROPE_AXIS_DIM)
    ang_r = (row[:, None] * inv)[:, None, None, :]
    ang_c = (col[:, None] * inv)[:, None, None, :]
    return jnp.cos(ang_r), jnp.sin(ang_r), jnp.cos(ang_c), jnp.sin(ang_c)


def rotate_half_pairs(x, cos, sin):
    x1, x2 = jnp.split(x, 2, axis=-1)
    cos = cos.astype(x.dtype)
    sin = sin.astype(x.dtype)
    return jnp.concatenate([x1 * cos - x2 * sin, x2 * cos + x1 * sin], axis=-1)


def apply_axial_rope(x, tables):
    cos_r, sin_r, cos_c, sin_c = tables
    xr, xc = jnp.split(x, 2, axis=-1)
    return jnp.concatenate([rotate_half_pairs(xr, cos_r, sin_r), rotate_half_pairs(xc, cos_c, sin_c)], axis=-1)


def project(h, w_in, q_g, k_g):
    b, n, _ = h.shape
    p = h @ w_in
    q, k, v, f = jnp.split(p, [QK_WIDTH, 2 * QK_WIDTH, 2 * QK_WIDTH + ATTN_WIDTH], axis=-1)
    q = rms_norm(q.reshape(b, n, N_HEADS, 2, HEAD_DIM), q_g)
    k = rms_norm(k.reshape(b, n, N_HEADS, 2, HEAD_DIM), k_g)
    v = v.reshape(b, n, N_HEADS, V_DIM)
    f = f.reshape(b, n, N_FOUR_GROUPS, FOUR_GROUP)
    return q, k, v, f


def diff_lambda(lq1, lk1, lq2, lk2, lam_init):
    l1 = jnp.exp(jnp.sum(lq1.astype(jnp.float32) * lk1.astype(jnp.float32)))
    l2 = jnp.exp(jnp.sum(lq2.astype(jnp.float32) * lk2.astype(jnp.float32)))
    return l1 - l2 + lam_init


def diff_attention(q, k, v, lam):
    b, nq, h, _, dh = q.shape
    nblk = nq // Q_BLOCK
    scale = 1.0 / math.sqrt(dh)
    qb = q.reshape(b, nblk, Q_BLOCK, h, 2, dh).transpose(1, 0, 2, 3, 4, 5)

    def one_block(qblk):
        s = jnp.einsum('bqhmd,bkhmd->bhmqk', qblk, k).astype(jnp.float32) * scale
        p = jax.nn.softmax(s, axis=-1)
        a = p[:, :, 0] - lam * p[:, :, 1]
        return jnp.einsum('bhqk,bkhe->bqhe', a.astype(v.dtype), v)

    o = lax.map(one_block, qb)
    return o.transpose(1, 0, 2, 3, 4).reshape(b, nq, h, V_DIM)


def fourier_mix(f, w_four):
    b, n, _, _ = f.shape
    spec = jnp.fft.fft2(f.astype(jnp.float32), axes=(1, 3), norm='ortho').real.astype(f.dtype)
    return jnp.einsum('bngc,gce->bnge', spec, w_four).reshape(b, n, FOUR_WIDTH)


def merge(attn, four, subln_g, lam_init, w_out):
    b, n = attn.shape[:2]
    a = rms_norm(attn, subln_g) * (1.0 - lam_init)
    return jnp.concatenate([a.reshape(b, n, ATTN_WIDTH), four], axis=-1) @ w_out


def sq_relu_mlp(h, w1, w2):
    return jnp.square(jax.nn.relu(h @ w1)) @ w2


def setup_inputs(seed: int = 0) -> dict:
    key = jax.random.key(seed)
    ks = jax.random.split(key, 24)
    f32 = jnp.float32
    nrm = lambda k, shape, s: (jax.random.normal(k, shape, f32) * s)
    return {
        'x_prompt': nrm(ks[0], (BATCH, SEQ, D_MODEL), 1.0),
        'x_sample': nrm(ks[1], (DEC_BATCH, DEC_SEQ, D_MODEL), 1.0),
        'c': nrm(ks[2], (DEC_BATCH, D_MODEL), 1.0),
        'cache_k': nrm(ks[3], (DEC_BATCH, DEPTH, PAST_LEN, N_HEADS, 2, HEAD_DIM), 1.0),
        'cache_v': nrm(ks[4], (DEC_BATCH, DEPTH, PAST_LEN, N_HEADS, V_DIM), 1.0),
        'c_ctx': nrm(ks[5], (D_MODEL,), 1.0),
        'w_mod': nrm(ks[6], (DEPTH, D_MODEL, 6 * D_MODEL), 0.5 * D_MODEL ** -0.5),
        'b_mod': nrm(ks[7], (DEPTH, 6 * D_MODEL), 0.02),
        'norm1_g': 1.0 + nrm(ks[8], (DEPTH, D_MODEL), 0.02),
        'w_in': nrm(ks[9], (DEPTH, D_MODEL, PROJ_WIDTH), D_MODEL ** -0.5),
        'q_norm_g': 1.0 + nrm(ks[10], (DEPTH, HEAD_DIM), 0.02),
        'k_norm_g': 1.0 + nrm(ks[11], (DEPTH, HEAD_DIM), 0.02),
        'lambda_q1': nrm(ks[12], (DEPTH, HEAD_DIM), 0.1),
        'lambda_k1': nrm(ks[13], (DEPTH, HEAD_DIM), 0.1),
        'lambda_q2': nrm(ks[14], (DEPTH, HEAD_DIM), 0.1),
        'lambda_k2': nrm(ks[15], (DEPTH, HEAD_DIM), 0.1),
        'subln_g': 1.0 + nrm(ks[16], (DEPTH, V_DIM), 0.02),
        'w_four': nrm(ks[17], (DEPTH, N_FOUR_GROUPS, FOUR_GROUP, FOUR_GROUP), FOUR_GROUP ** -0.5),
        'w_out': nrm(ks[18], (DEPTH, MIX_WIDTH, D_MODEL), MIX_WIDTH ** -0.5),
        'norm2_g': 1.0 + nrm(ks[19], (DEPTH, D_MODEL), 0.02),
        'w1': nrm(ks[20], (DEPTH, D_MODEL, D_FF), D_MODEL ** -0.5),
        'w2': nrm(ks[21], (DEPTH, D_FF, D_MODEL), D_FF ** -0.5),
    }


def reference(x_prompt, x_sample, c, cache_k, cache_v, c_ctx, w_mod, b_mod, norm1_g, w_in,
              q_norm_g, k_norm_g, lambda_q1, lambda_k1, lambda_q2, lambda_k2, subln_g,
              w_four, w_out, norm2_g, w1, w2):
    xp = x_prompt
    xs = x_sample
    tables = rope_tables(xs.shape[1])
    new_k, new_v = [], []
    for l in range(DEPTH):
        lam_init = 0.8 - 0.6 * math.exp(-0.3 * l)
        lam = diff_lambda(lambda_q1[l], lambda_k1[l], lambda_q2[l], lambda_k2[l], lam_init)

        sh1, sc1, g1, sh2, sc2, g2 = adaln(c_ctx[None, :], w_mod[l], b_mod[l])
        h = rms_norm(xp, norm1_g[l]) * (1 + sc1) + sh1
        q, k, v, f = project(h, w_in[l], q_norm_g[l], k_norm_g[l])
        attn = diff_attention(q, k, v, lam)
        mix = merge(attn, fourier_mix(f, w_four[l]), subln_g[l], lam_init, w_out[l])
        xp = xp + g1 * mix
        h = rms_norm(xp, norm2_g[l]) * (1 + sc2) + sh2
        xp = xp + g2 * sq_relu_mlp(h, w1[l], w2[l])
        new_k.append(k)
        new_v.append(v)

        sh1, sc1, g1, sh2, sc2, g2 = adaln(c, w_mod[l], b_mod[l])
        h = rms_norm(xs, norm1_g[l]) * (1 + sc1) + sh1
        q, k, v, f = project(h, w_in[l], q_norm_g[l], k_norm_g[l])
        q = apply_axial_rope(q, tables)
        k = apply_axial_rope(k, tables)
        k_all = jnp.concatenate([cache_k[:, l].astype(k.dtype), k], axis=1)
        v_all = jnp.concatenate([cache_v[:, l].astype(v.dtype), v], axis=1)
        attn = diff_attention(q, k_all, v_all, lam)
        mix = merge(attn, fourier_mix(f, w_four[l]), subln_g[l], lam_init, w_out[l])
        xs = xs + g1 * mix
        h = rms_norm(xs, norm2_g[l]) * (1 + sc2) + sh2
        xs = xs + g2 * sq_relu_mlp(h, w1[l], w2[l])

    new_cache_k = jnp.stack(new_k, axis=1)
    new_cache_v = jnp.stack(new_v, axis=1)
    return (xp, xs, new_cache_k, new_cache_v)
```

```python
import os
import numpy as np
import ml_dtypes
import concourse.bass as bass
import concourse.mybir as mybir
from concourse.bass_utils import run_bass_kernel_spmd

F32 = mybir.dt.float32
BF16 = mybir.dt.bfloat16
U8 = mybir.dt.uint8
AF = mybir.ActivationFunctionType
ALU = mybir.AluOpType
AX = mybir.AxisListType
EPS = 1e-6
LAM_INIT = 0.8 - 0.6 * 1.0


class Buf:
    __slots__ = ("name", "last_w", "readers", "rw")

    def __init__(self, name, rw=True):
        self.name = name
        self.last_w = None
        self.readers = []
        self.rw = rw


class DmaSem:
    __slots__ = ("sem", "count", "name")

    def __init__(self, name):
        self.name = name
        self.sem = None
        self.count = 0


class Op:
    __slots__ = ("eng", "fn", "idx", "eidx", "deps", "signal", "count", "dsem", "dcount", "is_dma", "fence", "raw", "snap")

    def __init__(self, eng, fn, idx):
        self.eng = eng
        self.fn = fn
        self.idx = idx
        self.eidx = 0
        self.deps = {}
        self.signal = False
        self.count = 0
        self.dsem = None
        self.dcount = 0
        self.is_dma = False
        self.fence = None
        self.snap = None


QUEUES = ("pe", "act", "dve", "pool", "sp")


class Sched:
    def __init__(self, nc):
        self.nc = nc
        self.ops = []
        self.eng_ops = {e: [] for e in QUEUES}
        self.dsems = []
        self.cur_fence = None
        self.marks = []
        self.waitlog = {}

    def mark(self, name):
        self.marks.append((name, len(self.eng_ops["pe"])))

    def buf(self, name, rw=True):
        return Buf(name, rw)

    def bufs(self, name, n, rw=True):
        return [Buf(f"{name}{i}", rw) for i in range(n)]

    def dsem(self, name):
        d = DmaSem(name)
        self.dsems.append(d)
        return d

    def fence(self):
        need = {}
        for e in QUEUES:
            for op in reversed(self.eng_ops[e]):
                if not op.is_dma:
                    need[e] = op
                    op.signal = True
                    break
        dneed = {d: d.count for d in self.dsems if d.count}
        self.cur_fence = (need, dneed)

    def snapshot(self):
        need = {}
        for e in QUEUES:
            for op in reversed(self.eng_ops[e]):
                if not op.is_dma:
                    need[e] = op
                    op.signal = True
                    break
        dneed = {d: d.count for d in self.dsems if d.count}
        return (need, dneed)

    def add(self, eng, fn, reads=(), writes=(), dsem=None, snap=None):
        op = Op(eng, fn, len(self.ops))
        op.snap = snap
        op.eidx = len(self.eng_ops[eng])
        op.fence = self.cur_fence
        if dsem is not None:
            op.is_dma = True
            op.dsem = dsem
            dsem.count += 16
            op.dcount = dsem.count
        deps = {}
        raw = set()
        for b in reads:
            o = b.last_w
            if o is not None:
                deps[o.idx] = o
                raw.add(o.idx)
        for b in writes:
            o = b.last_w
            if o is not None:
                deps[o.idx] = o
                if b.rw:
                    raw.add(o.idx)
            for r in b.readers:
                deps[r.idx] = r
        op.raw = raw
        for b in writes:
            b.last_w = op
            b.readers = []
        for b in reads:
            b.readers.append(op)
        deps.pop(op.idx, None)
        op.deps = deps
        self.ops.append(op)
        self.eng_ops[eng].append(op)
        return op

    def finalize(self):
        for op in self.ops:
            need = {}
            dneed = {}
            for fz in (op.fence, op.snap):
                if fz is None:
                    continue
                fn_, fd_ = fz
                for e, d in fn_.items():
                    if e == op.eng and not op.is_dma and (op.eng == "pe" or op.eidx - d.eidx > 4):
                        continue
                    if e not in need or need[e].eidx < d.eidx:
                        need[e] = d
                for k_, c_ in fd_.items():
                    if k_ not in dneed or dneed[k_] < c_:
                        dneed[k_] = c_
            for d in op.deps.values():
                if d.is_dma:
                    k = d.dsem
                    if k not in dneed or dneed[k] < d.dcount:
                        dneed[k] = d.dcount
                else:
                    if d.eng == op.eng and not op.is_dma:
                        if op.eng == "pe":
                            continue
                        if op.eidx - d.eidx > 4:
                            continue
                        if d.idx not in op.raw:
                            continue
                    if d.eng not in need or need[d.eng].eidx < d.eidx:
                        need[d.eng] = d
            op.deps = (need, dneed)
            for d in need.values():
                d.signal = True
        for e in QUEUES:
            c = 0
            for op in self.eng_ops[e]:
                if op.signal and not op.is_dma:
                    c += 1
                    op.count = c

    def emit(self):
        import contextlib
        nc = self.nc
        self.finalize()
        with contextlib.ExitStack() as st:
            esem = {e: st.enter_context(nc.semaphore(f"s_{e}")) for e in QUEUES}
            for d in self.dsems:
                d.sem = st.enter_context(nc.semaphore(f"d_{d.name}"))
            block = st.enter_context(nc.Block())

            def run(eng_name):
                def body(eng):
                    waited = {}
                    wl = self.waitlog.setdefault(eng_name, [])
                    for op in self.eng_ops[eng_name]:
                        need, dneed = op.deps
                        for e, d in need.items():
                            if waited.get(e, 0) < d.count:
                                eng.wait_ge(esem[e], d.count)
                                waited[e] = d.count
                                wl.append((e, d.fn.__code__.co_firstlineno, op.fn.__code__.co_firstlineno))
                        for ds, cnt in dneed.items():
                            if waited.get(ds, 0) < cnt:
                                eng.wait_ge(ds.sem, cnt)
                                waited[ds] = cnt
                                wl.append(("dma:" + ds.name, 0, op.fn.__code__.co_firstlineno))
                        ins = op.fn(eng)
                        if op.is_dma:
                            ins.then_inc(op.dsem.sem, 16)
                        elif op.signal:
                            ins.then_inc(esem[eng_name], 1)
                    if eng_name == "sp":
                        for ds in self.dsems:
                            if ds.count and waited.get(ds, 0) < ds.count:
                                eng.wait_ge(ds.sem, ds.count)
                return body

            block.tensor(run("pe"))
            block.scalar(run("act"))
            block.vector(run("dve"))
            block.gpsimd(run("pool"))
            block.sync(run("sp"))


class Arena:
    def __init__(self, nc, nbytes):
        self.t = nc.alloc_sbuf_tensor("arena", [128, nbytes], U8)
        self.nbytes = nbytes
        self.off = 0
        self.peak = 0

    def alloc(self, shape, dt):
        esz = 4 if dt == F32 else 2
        nb = int(np.prod(shape[1:])) * esz
        off = (self.off + 63) // 64 * 64
        assert off + nb <= self.nbytes, f"arena overflow {off + nb} > {self.nbytes}"
        self.off = off + nb
        self.peak = max(self.peak, self.off)
        if os.environ.get('MK_ALLOCLOG'):
            import traceback
            fr = traceback.extract_stack(limit=3)[0]
            print(f"ALLOC off={off:7d} nb={nb:6d} end={off+nb:7d} line={fr.lineno} {fr.line[:50]}")
        v = self.t[:, off:off + nb].bitcast(dt)
        if len(shape) == 3:
            v = v.rearrange("p (a b) -> p a b", b=shape[2])
        elif len(shape) == 4:
            v = v.rearrange("p (a b c) -> p a b c", b=shape[2], c=shape[3])
        return v


def build_nc(debug_phases=None):
    nc = bass.Bass("TRN2", target_bir_lowering=False)

    def din(name, shape, d=F32):
        return nc.dram_tensor(name, list(shape), d, kind="ExternalInput").ap()

    def dout(name, shape):
        return nc.dram_tensor(name, list(shape), F32, kind="ExternalOutput").ap()

    xp = din("xp", [1024, 1024])
    xs = din("xs", [2048, 1024])
    cvec = din("cvec", [2, 1024])
    ck = din("ck", [512, 512])
    cv = din("cv", [512, 512])
    w_mod = din("w_mod", [1024, 6144])
    b_mod = din("b_mod", [6144])
    norm1_g = din("norm1_g", [1024])
    w_in = din("w_in", [1024, 2048])
    qg = din("qg", [64])
    kg = din("kg", [64])
    lam4 = din("lam4", [256])
    subg = din("subg", [128])
    w_four = din("w_four", [4, 128, 128])
    w_out = din("w_out", [1024, 1024])
    norm2_g = din("norm2_g", [1024])
    w1 = din("w1", [1024, 4096])
    w2 = din("w2", [4096, 1024])
    ident_d = din("ident", [128, 128], BF16)
    ccs_d = din("ccs", [128, 256], BF16)
    dftp_d = din("dftp", [128, 1024], BF16)
    dfts_d = din("dfts", [2048, 2, 1024], BF16)
    rope_d = din("rope", [128, 1024])
    ident32_d = din("ident32", [64, 64])
    yp = dout("yp", [1024, 1024])
    ys = dout("ys", [1024, 1024])
    nk = dout("nk", [1024, 512])
    nv = dout("nv", [1024, 512])

    S = Sched(nc)
    A = Arena(nc, 212480)

    banks = []
    for i in range(8):
        t = nc.alloc_psum_tensor(f"bank{i}", [128, 512], F32)
        banks.append((t[:], t[:].bitcast(BF16), S.buf(f"bank{i}", rw=False)))
    bank_ctr = [0]
    reserved = set()
    alloc_stamp = [-1.0] * 8

    def nb():
        assert len(reserved) < 8, "no free PSUM bank"
        best, bestk = None, None
        for i in range(8):
            if i in reserved:
                continue
            lw = banks[i][2].last_w
            k = max(-1.0 if lw is None else float(lw.idx), alloc_stamp[i])
            if bestk is None or k < bestk:
                best, bestk = i, k
        bank_ctr[0] += 1
        alloc_stamp[best] = len(S.ops) + bank_ctr[0] * 1e-7
        return banks[best] + (best,)

    uid = [0]

    def uname(p):
        uid[0] += 1
        return f"{p}{uid[0]}"

    ident = A.alloc([128, 128], BF16)
    ccs = A.alloc([128, 256], BF16)
    wfour = A.alloc([128, 4, 128], BF16)
    wcs = A.alloc([128, 4, 256], BF16)
    dftp = A.alloc([128, 2, 2, 256], BF16)
    rope = A.alloc([128, 16, 2, 32], F32)
    gqB = A.alloc([128, 64], F32)
    gkB = A.alloc([128, 64], F32)
    gqcol = A.alloc([128, 2], F32)
    subgB = A.alloc([128, 128], F32)
    lamB = A.alloc([128, 4, 64], F32)
    lamt = A.alloc([128, 2, 64], F32)
    lams = A.alloc([128, 8], F32)
    colv = A.alloc([128, 64], F32)
    V64 = A.alloc([128, 128], F32)
    ident32 = A.alloc([128, 64], F32)
    cT = colv[:, 0:16].rearrange("p (v k) -> p v k", k=8)
    sil = A.alloc([128, 2, 8], F32)
    silb = A.alloc([128, 8, 2], BF16)
    screp = A.alloc([128, 2, 8, 128], BF16)
    ncol = colv[:, 16:32].rearrange("p (v k) -> p v k", k=8)
    bmcol = colv[:, 32:64].rearrange("p (v k) -> p v k", k=8)
    modcol = A.alloc([128, 4, 8, 2], F32)
    gmod = A.alloc([128, 2, 8, 2], F32)
    gB = A.alloc([128, 2, 2, 1024], F32)
    stat = A.alloc([128, 64], F32)
    junk = A.alloc([128, 1024], BF16)
    epsT = A.alloc([128, 8], F32)
    CAT = A.alloc([128, 8, 1024], BF16)
    cat_off = A.off - 8 * 1024 * 2
    if os.environ.get('MK_MARKS'):
        print('CAT_OFF', cat_off)
    phase_base = A.off

    dconst = S.dsem("const")
    b_const = S.buf("const")
    const_ops = []

    dcrit = S.dsem("crit")
    b_crit = S.buf("crit")
    crit_ops = []

    def cload(eng, out, in_, crit=False, **kw):
        if crit:
            crit_ops.append(S.add(eng, lambda e: e.dma_start(out=out, in_=in_, **kw), writes=[], dsem=dcrit))
        else:
            const_ops.append(S.add(eng, lambda e: e.dma_start(out=out, in_=in_, **kw), writes=[], dsem=dconst))

    cload("sp", V64[0:16, :], cvec.rearrange("v (k c) -> (v k) c", c=128), crit=True)
    cload("sp", ident32[0:64, :], ident32_d, crit=True)
    cload("sp", V64[16:24, :], norm1_g.rearrange("(k c) -> k c", c=128), crit=True)
    cload("sp", V64[24:32, :], norm2_g.rearrange("(k c) -> k c", c=128), crit=True)
    for jj, j in enumerate((0, 1, 3, 4)):
        cload("sp", V64[32 + jj * 8: 40 + jj * 8, :], b_mod[j * 1024:(j + 1) * 1024].rearrange("(k c) -> k c", c=128), crit=True)
    cload("sp", ident, ident_d)
    cload("sp", ccs, ccs_d)
    d_tab = S.dsem("tabs")
    b_dftp, b_rope = S.buf("dftp"), S.buf("rope")
    cload("sp", gqB, qg.partition_broadcast(128))
    cload("sp", gkB, kg.partition_broadcast(128))
    cload("sp", gqcol[0:64, 0:1], qg.rearrange("(p o) -> p o", o=1))
    cload("sp", gqcol[64:128, 0:1], qg.rearrange("(p o) -> p o", o=1))
    cload("sp", subgB, subg.partition_broadcast(128))
    cload("sp", lamB.rearrange("p a b -> p (a b)"), lam4.partition_broadcast(128))
    d_wf = S.dsem("wfour")
    b_wf = S.buf("wfour")
    S.add("pool", lambda e: e.dma_start(out=wfour, in_=w_four.rearrange("g c e -> c g e")), writes=[b_wf], dsem=d_wf)
    for o in const_ops:
        o.dcount = dconst.count
    b_const.last_w = const_ops[-1]
    _t1 = S.add("sp", lambda e: e.dma_start(out=dftp, in_=dftp_d.rearrange("p (t c n) -> p t c n", t=2, c=2)), writes=[b_dftp], dsem=d_tab)
    _t2 = S.add("sp", lambda e: e.dma_start(out=rope, in_=rope_d.rearrange("p (t c f) -> p t c f", t=16, c=2)), writes=[b_rope], dsem=d_tab)
    _t1.dcount = d_tab.count
    _t2.dcount = d_tab.count
    b_colv = S.buf("colv")
    _pf, _pb, _bb, _bi = nb()
    for o in crit_ops:
        o.dcount = dcrit.count
    b_crit.last_w = crit_ops[-1]
    S.add("pe", lambda e: e.transpose(out=_pf[:, 0:64], in_=V64[0:64, :], identity=ident32[0:64, 0:64]),
          reads=[b_crit], writes=[_bb])
    S.add("dve", lambda e: e.tensor_copy(out=colv, in_=_pf[:, 0:64]), reads=[], writes=[_bb, b_colv])
    RC = [b_const, b_colv]

    def add(eng, fn, reads=(), writes=()):
        return S.add(eng, fn, reads=list(reads), writes=list(writes))

    b_lam = S.buf("lam")
    add("dve", lambda e: e.tensor_tensor(out=lamt, in0=lamB[:, 0::2, :], in1=lamB[:, 1::2, :], op=ALU.mult),
        reads=RC, writes=[b_lam])
    add("dve", lambda e: e.tensor_reduce(out=lams[:, 0:2], in_=lamt, axis=AX.X, op=ALU.add), reads=[b_lam], writes=[b_lam])
    add("act", lambda e: e.activation(out=lams[:, 0:2], in_=lams[:, 0:2], func=AF.Exp), reads=[b_lam], writes=[b_lam])
    add("dve", lambda e: e.tensor_tensor(out=lams[:, 2:3], in0=lams[:, 0:1], in1=lams[:, 1:2], op=ALU.subtract),
        reads=[b_lam], writes=[b_lam])
    add("dve", lambda e: e.tensor_scalar(out=lams[:, 3:4], in0=lams[:, 2:3], scalar1=LAM_INIT, scalar2=-1.0,
                                         op0=ALU.add, op1=ALU.mult), reads=[b_lam], writes=[b_lam])
    NLAM = lams[:, 3:4]
    b_eps = S.buf('eps')
    add('dve', lambda e: e.memset(epsT, EPS), reads=[], writes=[b_eps])
    b_subg = S.buf("subg")
    add("dve", lambda e: e.tensor_scalar(out=subgB, in0=subgB, scalar1=1.0 - LAM_INIT, scalar2=None, op0=ALU.mult),
        reads=RC, writes=[b_subg])
    b_wcs = S.buf("wcs")
    for gp in range(2):
        pf, pb, bb, _bi = nb()
        for gi in range(2):
            g = gp * 2 + gi
            for cs in range(2):
                add("pe", lambda e, g=g, gi=gi, cs=cs, pf=pf: e.matmul(
                    pf[:, gi * 256 + cs * 128: gi * 256 + (cs + 1) * 128], lhsT=ccs[:, cs * 128:(cs + 1) * 128],
                    rhs=wfour[:, g, :], start=True, stop=True), reads=RC + [b_wf], writes=[bb])
        add("dve", lambda e, gp=gp, pf=pf: e.tensor_copy(
            out=wcs[:, gp * 2:gp * 2 + 2, :], in_=pf.rearrange("p (a b) -> p a b", b=256)),
            reads=[], writes=[bb, b_wcs])

    b_sil = S.buf("sil")
    add("act", lambda e: e.activation(out=sil, in_=cT, func=AF.Silu), reads=[b_colv], writes=[b_sil])
    add("dve", lambda e: e.tensor_copy(out=silb, in_=sil.rearrange("p v k -> p k v")), reads=[b_sil], writes=[b_sil])
    add("dve", lambda e: e.tensor_copy(out=screp.rearrange("p v k m -> p (v k) m"),
                                       in_=sil.rearrange("p v k -> p (v k)").unsqueeze(2).broadcast_to([128, 16, 128])),
        reads=[b_sil], writes=[b_sil])

    mod_mark = A.off
    WIN_S_OFF = mod_mark + 8192 + 20480 + 20800 + 16384
    WM = [A.alloc([128, 8, 512], BF16) for _ in range(2)]
    bgt = A.alloc([128, 512], F32)
    b_wm = S.bufs("wm", 2)
    d_wm = [S.dsem(f"wm{i}") for i in range(2)]
    d_bg = S.dsem("bg")
    b_bg = S.buf("bg")
    wm_ctr = [0]
    b_mod_ = S.buf("modcol")
    b_gB = S.buf("gB")

    def load_wm(col0):
        s_ = wm_ctr[0] % 2
        wm_ctr[0] += 1
        S.add("pool", lambda e: e.dma_start(out=WM[s_], in_=w_mod[:, col0:col0 + 512].rearrange("(k p) n -> p k n", p=128)),
              writes=[b_wm[s_]], dsem=d_wm[s_])
        return s_

    def mod_vec_steps(jj, j):
        hold = {}

        def mk(half):
            def load():
                hold[("s", half)] = load_wm(j * 1024 + half * 512)

            def comp():
                if half == 0:
                    hold["b"] = nb()
                    reserved.add(hold["b"][3])
                pf, pb, bb, _bi = hold["b"]
                s_ = hold[("s", half)]
                for fc in range(4):
                    fcg = half * 4 + fc
                    for k in range(8):
                        add("pe", lambda e, s_=s_, fc=fc, fcg=fcg, k=k, pf=pf: e.matmul(
                            pf[:, fcg * 2:fcg * 2 + 2], lhsT=WM[s_][:, k, fc * 128:(fc + 1) * 128], rhs=silb[:, k, :],
                            start=(k == 0), stop=(k == 7)), reads=[b_wm[s_], b_sil], writes=[bb])
                if half == 1:
                    add("dve", lambda e, pf=pf: e.tensor_tensor(
                        out=modcol[:, jj, :, :], in0=pf[:, 0:16].rearrange("p (a b) -> p a b", b=2),
                        in1=bmcol[:, jj, :].unsqueeze(2).broadcast_to([128, 8, 2]), op=ALU.add), reads=RC, writes=[bb, b_mod_])
                    reserved.discard(hold["b"][3])
            return (load, comp)
        return [mk(0), mk(1)]

    def mod_gate_steps(gi, j):
        hold = {}

        def mk(half):
            def load():
                hold[half] = load_wm(j * 1024 + half * 512)

            def comp():
                s_ = hold[half]
                S.add("sp", lambda e: e.dma_start(
                    out=bgt, in_=b_mod[j * 1024 + half * 512: j * 1024 + (half + 1) * 512].partition_broadcast(128)),
                    writes=[b_bg], dsem=d_bg)
                for v in range(2):
                    pf, pb, bb, _bi = nb()
                    for k in range(8):
                        add("pe", lambda e, s_=s_, v=v, k=k, pf=pf: e.matmul(
                            pf, lhsT=screp[:, v, k, :], rhs=WM[s_][:, k, :], start=(k == 0), stop=(k == 7)),
                            reads=[b_wm[s_], b_sil], writes=[bb])
                    add("dve", lambda e, v=v, pf=pf: e.tensor_tensor(
                        out=gB[:, v, gi, half * 512:(half + 1) * 512], in0=pf, in1=bgt, op=ALU.add),
                        reads=[b_bg], writes=[bb, b_gB])
            return (load, comp)
        return [mk(0), mk(1)]

    def mod_finish(n):
        add("dve", lambda e: e.scalar_tensor_tensor(
            out=gmod[:, n, :, :], in0=modcol[:, 2 * n + 1, :, :], scalar=1.0,
            in1=ncol[:, n, :].unsqueeze(2).broadcast_to([128, 8, 2]), op0=ALU.add, op1=ALU.mult),
            reads=RC + [b_mod_], writes=[b_mod_])

    S.mark('mod')
    v34 = mod_vec_steps(3, 4)
    v23 = mod_vec_steps(2, 3)
    msteps = mod_vec_steps(1, 1) + mod_vec_steps(0, 0) + [(lambda: None, lambda: mod_finish(0))] + \
        mod_gate_steps(0, 2) + v34 + v23 + [(lambda: None, lambda: mod_finish(1))] + mod_gate_steps(1, 5)
    mod_calls = []
    for i_ in range(len(msteps)):
        def call(i_=i_):
            if i_ + 1 < len(msteps):
                msteps[i_ + 1][0]()
            msteps[i_][1]()
        mod_calls.append(call)
    msteps[0][0]()
    for _ in range(5):
        mod_calls.pop(0)()
    RM = [b_mod_]
    late_mod = mod_calls

    def rstd_ops(src, dst, inv_n, b_stat):
        add("act", lambda e: e.activation(out=dst, in_=src, func=AF.Ln, scale=inv_n, bias=epsT[:, 0:1]),
            reads=[b_stat, b_eps], writes=[b_stat])
        add("act", lambda e: e.activation(out=dst, in_=dst, func=AF.Exp, scale=-0.5), reads=[b_stat], writes=[b_stat])

    class Pipe:
        pass

    def make_pipe(nxs, with_qk, nht=2, nxn=2):
        P = Pipe()
        P.XS = [A.alloc([128, 1024], F32) for _ in range(nxs)]
        P.b_xs = S.bufs("xs", nxs)
        P.d_xs = [S.dsem(uname("xs")) for i in range(nxs)]
        P.XN = [A.alloc([128, 1024], BF16) for _ in range(nxn)]
        P.b_xn = S.bufs("xn", max(nxn, 1))
        P.HT = [A.alloc([128, 8, 512], BF16) for _ in range(nht)]
        P.b_ht = [[[S.buf("ht", rw=False) for _ in range(2)] for _ in range(4)] for _ in range(nht)]
        P.st = [A.alloc([128, 4], F32) for _ in range(4)]
        P.b_st = S.bufs("st", 4)
        P.ctr = 0
        P.gctr = 0
        P.actr = 0
        return P

    def norm_transpose(P, src_tile, b_src, t, hslot, n, v):
        c = P.ctr
        P.ctr += 1
        st = P.st[c % 4]
        bst = P.b_st[c % 4]
        xn = P.XN[c % 2]
        bxn = P.b_xn[c % 2]
        add("act", lambda e: e.activation(out=junk, in_=src_tile, func=AF.Square, accum_out=st[:, 0:1]),
            reads=[b_src], writes=[bst])
        rstd_ops(st[:, 0:1], st[:, 1:2], 1.0 / 1024, bst)
        add("dve", lambda e: e.tensor_scalar(out=xn, in0=src_tile, scalar1=st[:, 1:2], scalar2=None, op0=ALU.mult),
            reads=[b_src, bst], writes=[bxn])
        pf, pb, bb, _bi = nb()
        for k in range(8):
            add("pe", lambda e, k=k: e.transpose(out=pb[:, k * 128:(k + 1) * 128], in_=xn[:, k * 128:(k + 1) * 128],
                                                 identity=ident), reads=[bxn] + RC, writes=[bb])
        ht = P.HT[hslot]
        for k in range(8):
            par = k % 2
            bh = P.b_ht[hslot][t][par]
            if par == 0:
                add("act", lambda e, k=k: e.activation(
                    out=ht[:, k, t * 128:(t + 1) * 128], in_=pb[:, k * 128:(k + 1) * 128], func=AF.Identity,
                    scale=gmod[:, n, k, v:v + 1], bias=modcol[:, 2 * n, k, v:v + 1]), reads=RM, writes=[bb, bh])
            else:
                add("dve", lambda e, k=k: e.tensor_scalar(
                    out=ht[:, k, t * 128:(t + 1) * 128], in0=pb[:, k * 128:(k + 1) * 128],
                    scalar1=gmod[:, n, k, v:v + 1], scalar2=modcol[:, 2 * n, k, v:v + 1], op0=ALU.mult, op1=ALU.add),
                    reads=RM, writes=[bb, bh])

    def attention_multi(T, jobs):
        steps = []
        for ji, J in enumerate(jobs):
            nkt = len(J["kts"])
            G = max(1, 512 // J["nq"])
            for kt0 in range(0, nkt, G):
                ktl = list(range(kt0, min(nkt, kt0 + G)))
                for m in range(2):
                    steps.append((ji, ktl, m, ktl[-1] == nkt - 1 and m == 1))
        jst = {}

        def job_state(ji):
            if ji in jst:
                return jst[ji]
            J = jobs[ji]
            nq = J["nq"]
            st = {}
            qs = T.qctr % 2
            T.qctr += 1
            st["qp"] = T.QP[qs]
            st["bqp"] = T.b_qp[qs]
            for m in range(2):
                add("pool", lambda e, m=m, qp=st["qp"], q=J["q"], nq=nq: e.tensor_copy(
                    out=qp[m][m * 64:(m + 1) * 64, 0:nq], in_=q[m * 64:(m + 1) * 64, :]),
                    reads=[T.b_q], writes=[st["bqp"][m]])
            jst[ji] = st
            return st

        def alloc_acc(ji):
            st = jst[ji]
            nqt = jobs[ji]["nq"] // 128
            nbk = (nqt * 2 + 2) // 3
            st["accb"] = [nb() for _ in range(nbk)]
            for b_ in st["accb"]:
                reserved.add(b_[3])
            st["started"] = [False] * nbk

        stb = {}

        def emit_st(i):
            ji, ktl, m, _l = steps[i]
            st = job_state(ji)
            J = jobs[ji]
            nq = J["nq"]
            pf, pb, bb, _bi = nb()
            for l, kt in enumerate(ktl):
                kT, bk, vx, bv = J["kts"][kt]
                add("pe", lambda e, kT=kT, pf=pf, qp=st["qp"][m], nq=nq, l=l: e.matmul(
                    pf[:, l * nq:(l + 1) * nq], lhsT=kT, rhs=qp[:, 0:nq], start=True, stop=True),
                    reads=[bk, st["bqp"][m]], writes=[bb])
            stb[i] = (pf, bb, _bi)
            reserved.add(_bi)

        DEPTH = int(os.environ.get('MK_DEPTH', '3'))
        for i0 in range(min(DEPTH, len(steps))):
            emit_st(i0)
        for i, (ji, ktl, m, lastj) in enumerate(steps):
            if i + DEPTH < len(steps):
                emit_st(i + DEPTH)
            J = jobs[ji]
            nq = J["nq"]
            nqt = nq // 128
            st = jst[ji]
            if "accb" not in st:
                alloc_acc(ji)
                if ji + 1 < len(jobs):
                    job_state(ji + 1)
            pf, bb, _sbi = stb.pop(i)
            ps = T.pctr % len(T.PT)
            T.pctr += 1
            pt = T.PT[ps]
            bpt = T.b_pt[ps]
            wcols = len(ktl) * nq
            add("act", lambda e, pf=pf, pt=pt, wcols=wcols: e.activation(out=pt[:, 0:wcols], in_=pf[:, 0:wcols], func=AF.Exp, scale=0.125),
                reads=[], writes=[bb, bpt])
            reserved.discard(_sbi)
            for qt in range(nqt):
                a = qt * 2 + m
                bi, ci = a // 3, (a % 3) * 129
                af, _, ab, _x = st["accb"][bi]
                for l, kt in enumerate(ktl):
                    kT, bk, vx, bv = J["kts"][kt]
                    lastk = (kt == len(J["kts"]) - 1)
                    st_flag = not st["started"][bi]
                    st["started"][bi] = True
                    add("pe", lambda e, af=af, ci=ci, pt=pt, qt=qt, vx=vx, st_flag=st_flag, lastk=lastk, l=l, nq=nq: e.matmul(
                        af[:, ci:ci + 129], lhsT=pt[:, l * nq + qt * 128: l * nq + (qt + 1) * 128], rhs=vx,
                        start=st_flag, stop=lastk, skip_group_check=True), reads=[bpt, bv], writes=[ab])
            if lastj:
                pend = att_post(T, J, st)
                att_flush_evacs(T)
                if T.pending is not None:
                    att_post2(T, T.pending)
                T.pending = pend
                yield
        if T.pending is not None:
            att_post2(T, T.pending)
            T.pending = None
        att_flush_evacs(T)
        yield

    def att_post(T, J, st):
        nq, h = J["nq"], J["h"]
        nqt = nq // 128
        accb = st["accb"]
        sc = T.sctr % len(T.ast)
        T.sctr += 1
        ss = T.ast[sc]
        bs = T.b_ast[sc]
        aset = T.acctr % 2
        T.acctr += 1
        accs = []
        for bi, (af, _, ab, _x) in enumerate(accb):
            na = min(3, nqt * 2 - bi * 3)
            sb = T.ACCS[aset][bi]
            bsb = T.b_accs[aset][bi]
            if T.acc_eng == "act":
                add("act", lambda e, af=af, sb=sb, na=na: e.copy(out=sb[:, 0:129 * na], in_=af[:, 0:129 * na]),
                    reads=[], writes=[ab, bsb])
            else:
                add("dve", lambda e, af=af, sb=sb, na=na: e.tensor_copy(out=sb[:, 0:129 * na], in_=af[:, 0:129 * na]),
                    reads=[], writes=[ab, bsb])
            accs.append((sb, bsb))
        for b_ in accb:
            reserved.discard(b_[3])
        for bi, (sb, bsb) in enumerate(accs):
            na = min(3, nqt * 2 - bi * 3)
            add("dve", lambda e, sb=sb, bi=bi, na=na, ss=ss: e.reciprocal(
                out=ss[:, bi * 3: bi * 3 + na], in_=sb[:, 128:128 + 129 * (na - 1) + 1:129]), reads=[bsb], writes=[bs])
        add("dve", lambda e, ss=ss: e.tensor_tensor(
            out=ss[:, 8:8 + nqt], in0=ss[:, 1:2 * nqt:2], in1=NLAM.broadcast_to([128, nqt]), op=ALU.mult),
            reads=[b_lam], writes=[bs])
        oos = []
        for qt in range(nqt):
            a0, a1 = qt * 2, qt * 2 + 1
            sb0, bsb0 = accs[a0 // 3]
            sb1, bsb1 = accs[a1 // 3]
            c0, c1 = (a0 % 3) * 129, (a1 % 3) * 129
            oc = T.octr % len(T.OO)
            T.octr += 1
            O0 = T.O0[oc % len(T.O0)]
            bo0 = T.b_o0[oc % len(T.O0)]
            OO, boo = T.OO[oc], T.b_oo[oc]
            add("dve", lambda e, sb0=sb0, c0=c0, ss=ss, O0=O0, a0=a0: e.tensor_scalar(
                out=O0, in0=sb0[:, c0:c0 + 128], scalar1=ss[:, a0:a0 + 1], scalar2=None, op0=ALU.mult),
                reads=[bs, bsb0], writes=[bo0])
            add("dve", lambda e, sb1=sb1, c1=c1, ss=ss, O0=O0, OO=OO, qt=qt: e.scalar_tensor_tensor(
                out=OO, in0=sb1[:, c1:c1 + 128], scalar=ss[:, 8 + qt:9 + qt], in1=O0, op0=ALU.mult, op1=ALU.add),
                reads=[bs, bo0, bsb1], writes=[boo])
            add("pool", lambda e, OO=OO, O0=O0: e.tensor_tensor(out=O0, in0=OO, in1=OO, op=ALU.mult),
                reads=[boo], writes=[bo0])
            add("dve", lambda e, O0=O0, ss=ss, qt=qt: e.tensor_reduce(out=ss[:, 12 + qt:13 + qt], in_=O0, axis=AX.X, op=ALU.add),
                reads=[bo0], writes=[bs])
            oos.append((OO, boo))
        return dict(J=J, ss=ss, bs=bs, oos=oos)

    def att_post2(T, pend):
        J, ss, bs, oos = pend["J"], pend["ss"], pend["bs"], pend["oos"]
        nq, h, cat_col0 = J["nq"], J["h"], J["col0"]
        nqt = nq // 128
        add("act", lambda e: e.activation(out=ss[:, 16:16 + nqt], in_=ss[:, 12:12 + nqt], func=AF.Ln, scale=1.0 / 128,
                                          bias=epsT[:, 0:1]), reads=[bs, b_eps], writes=[bs])
        add("act", lambda e: e.activation(out=ss[:, 16:16 + nqt], in_=ss[:, 16:16 + nqt], func=AF.Exp, scale=-0.5),
            reads=[bs], writes=[bs])
        for qt in range(nqt):
            OO, boo = oos[qt]
            at = T.AT[(T.actr + qt) % len(T.AT)]
            bat = T.b_at[(T.actr + qt) % len(T.AT)]
            add(T.fin_eng, lambda e, OO=OO, at=at, qt=qt: e.scalar_tensor_tensor(
                out=at[:, h * 128:(h + 1) * 128], in0=OO, scalar=ss[:, 16 + qt:17 + qt], in1=subgB, op0=ALU.mult, op1=ALU.mult),
                reads=[boo, bs, b_subg], writes=[bat])
        if h == 3:
            for qt in range(nqt):
                at = T.AT[(T.actr + qt) % len(T.AT)]
                bat = T.b_at[(T.actr + qt) % len(T.AT)]
                pf, pb, bb, _bi = nb()
                for hh in range(4):
                    add("pe", lambda e, hh=hh, at=at, pb=pb: e.transpose(
                        out=pb[:, hh * 128:(hh + 1) * 128], in_=at[:, hh * 128:(hh + 1) * 128], identity=ident),
                        reads=[bat] + RC, writes=[bb])
                col = cat_col0 + qt * 128
                reserved.add(_bi)
                T.evacs.append((pb, bb, _bi, col))
            T.actr += nqt

    def att_flush_evacs(T):
        while T.evacs:
            pb, bb, _bi, col = T.evacs.pop(0)
            add("dve", lambda e, pb=pb, col=col: e.tensor_copy(
                out=CAT[:, 0:4, col:col + 128], in_=pb[:, 0:512].rearrange("p (a b) -> p a b", b=128)),
                reads=[], writes=[bb, T.b_cat])
            reserved.discard(_bi)

    class Att:
        pass

    def make_att(npt, nat, noo, nq=512):
        T = Att()
        T.PT = [A.alloc([128, 512], BF16) for _ in range(npt)]
        T.b_pt = S.bufs("pt", npt)
        T.pctr = 0
        nbk = (nq // 128 * 2 + 2) // 3
        T.ACCS = [[A.alloc([128, 388], F32) for _ in range(nbk)] for _ in range(2)]
        T.b_accs = [S.bufs("accs", nbk) for _ in range(2)]
        T.acctr = 0
        T.ast = [A.alloc([128, 32], F32) for _ in range(4)]
        T.b_ast = S.bufs("ast", 4)
        T.sctr = 0
        T.O0 = [A.alloc([128, 128], F32) for _ in range(2)]
        T.OO = [A.alloc([128, 128], F32) for _ in range(noo)]
        T.b_o0 = S.bufs("o0", 2)
        T.b_oo = S.bufs("oo", noo)
        T.octr = 0
        T.pending = None
        T.evacs = []
        T.acc_eng = 'dve'
        T.fin_eng = 'dve'
        T.AT = [A.alloc([128, 512], BF16) for _ in range(nat)]
        T.b_at = S.bufs("at", nat, rw=False)
        T.actr = 0
        T.b_cat = S.buf("cat", rw=False)
        T.b_q = S.buf("qT")
        T.QP = [[A.alloc([128, nq], BF16) for _ in range(2)] for _ in range(2)]
        T.b_qp = [S.bufs("qp", 2) for _ in range(2)]
        T.qctr = 0
        for qs in range(2):
            for m in range(2):
                add("pool", lambda e, qs=qs, m=m: e.memset(T.QP[qs][m], 0.0), reads=[], writes=[T.b_qp[qs][m]])
        return T

    class QK:
        pass

    def make_qk(with_rope=True, nvo=2, nko=2):
        Q = QK()
        Q.SQ = [A.alloc([128, 512], F32) for _ in range(2)]
        Q.b_sq = S.bufs("sq", 2)
        Q.QN = [A.alloc([128, 512], F32) for _ in range(2)]
        Q.b_qn = S.bufs("qn", 2)
        Q.QB = [A.alloc([128, 512], BF16) for _ in range(6)]
        Q.b_qb = S.bufs("qb", 6)
        Q.bctr = 0
        Q.RT = [A.alloc([128, 256], F32) for _ in range(4)] if with_rope else []
        Q.b_rt = S.bufs("rt", 4)
        Q.st = [A.alloc([128, 16], F32) for _ in range(2)]
        Q.b_st = S.bufs("qst", 2)
        Q.KO = [A.alloc([128, 512], F32) for _ in range(nko)]
        Q.b_ko = S.bufs("ko", max(nko, 1))
        Q.d_ko = [S.dsem(uname("ko")) for i in range(nko)]
        Q.VO = [A.alloc([128, 512], F32) for _ in range(nvo)]
        Q.b_vo = S.bufs("vo", max(nvo, 1))
        Q.d_vo = [S.dsem(uname("vo")) for i in range(nvo)]
        Q.ctr = 0
        Q.kctr = 0
        Q.vctr = 0
        return Q

    def qk_D(Q, bank, gBc, rope_tile, kout=None):
        pf, _, bb, _x = bank
        c = Q.ctr % 2
        Q.ctr += 1
        cb = Q.bctr % 6
        Q.bctr += 1
        sq, bsq = Q.SQ[c], Q.b_sq[c]
        qn, bqn = Q.QN[c], Q.b_qn[c]
        qb, bqb = Q.QB[cb], Q.b_qb[cb]
        st, bst = Q.st[c], Q.b_st[c]
        add("act", lambda e: e.activation(out=sq, in_=pf, func=AF.Square), reads=[], writes=[bb, bsq])
        add("dve", lambda e: e.tensor_reduce(out=st[:, 0:8], in_=sq.rearrange("p (a d) -> p a d", d=64), axis=AX.X, op=ALU.add),
            reads=[bsq], writes=[bst])
        rstd_ops(st[:, 0:8], st[:, 8:16], 1.0 / 64, bst)
        add("dve", lambda e: e.tensor_tensor(
            out=qn.rearrange("p (a d) -> p a d", d=64), in0=pf.rearrange("p (a d) -> p a d", d=64),
            in1=st[:, 8:16].unsqueeze(2).broadcast_to([128, 8, 64]), op=ALU.mult), reads=[bst], writes=[bb, bqn])
        gbc = gBc.unsqueeze(1).broadcast_to([128, 8, 64])
        if rope_tile is None:
            if kout is not None:
                kc = Q.kctr % len(Q.KO)
                Q.kctr += 1
                ko, bko, dko = Q.KO[kc], Q.b_ko[kc], Q.d_ko[kc]
                add("pool", lambda e: e.tensor_tensor(out=ko.rearrange("p (a d) -> p a d", d=64),
                                                      in0=qn.rearrange("p (a d) -> p a d", d=64), in1=gbc, op=ALU.mult),
                    reads=[bqn] + RC, writes=[bko])
                S.add("sp", lambda e: e.dma_start(out=kout, in_=ko), reads=[bko], dsem=dko)
                add("act", lambda e: e.copy(out=qb, in_=ko), reads=[bko], writes=[bqb])
            else:
                add("dve", lambda e: e.tensor_copy(out=qb, in_=qn), reads=[bqn], writes=[bqb])
        else:
            add("pool", lambda e: e.tensor_tensor(out=qn.rearrange("p (a d) -> p a d", d=64),
                                                  in0=qn.rearrange("p (a d) -> p a d", d=64), in1=gbc, op=ALU.mult),
                reads=RC, writes=[bqn])
            xv = qn.rearrange("p (h x j f) -> p h x j f", h=8, x=2, j=2, f=16)
            ov = qb.rearrange("p (h x j f) -> p h x j f", h=8, x=2, j=2, f=16)
            X1, X2 = xv[:, :, :, 0, :], xv[:, :, :, 1, :]
            O1, O2 = ov[:, :, :, 0, :], ov[:, :, :, 1, :]
            cosB = rope[:, rope_tile, 0, :].rearrange("p (x f) -> p x f", f=16).unsqueeze(1).broadcast_to([128, 8, 2, 16])
            sinB = rope[:, rope_tile, 1, :].rearrange("p (x f) -> p x f", f=16).unsqueeze(1).broadcast_to([128, 8, 2, 16])
            T1, T2, T3, T4 = [r.rearrange("p (h x f) -> p h x f", h=8, x=2, f=16) for r in Q.RT]
            b1, b2, b3, b4 = Q.b_rt
            add("dve", lambda e: e.tensor_tensor(out=T1, in0=X1, in1=cosB, op=ALU.mult), reads=[bqn, b_rope] + RC, writes=[b1])
            add("pool", lambda e: e.tensor_tensor(out=T2, in0=X2, in1=sinB, op=ALU.mult), reads=[bqn, b_rope] + RC, writes=[b2])
            add("pool", lambda e: e.tensor_tensor(out=T3, in0=X2, in1=cosB, op=ALU.mult), reads=[bqn, b_rope] + RC, writes=[b3])
            add("dve", lambda e: e.tensor_tensor(out=T4, in0=X1, in1=sinB, op=ALU.mult), reads=[bqn, b_rope] + RC, writes=[b4])
            add("dve", lambda e: e.tensor_tensor(out=O1, in0=T1, in1=T2, op=ALU.subtract), reads=[b1, b2], writes=[bqb])
            add("pool", lambda e: e.tensor_tensor(out=O2, in0=T3, in1=T4, op=ALU.add), reads=[b3, b4], writes=[bqb])
        return qb, bqb

    def qk_E(items):
        tf, tb, tbb, _bi = nb()
        for ii, (qb, bqb, dstT, b_dst, _sc) in enumerate(items):
            for hh in range(4):
                add("pe", lambda e, hh=hh, ii=ii, qb=qb: e.transpose(
                    out=tb[:, ii * 512 + hh * 128: ii * 512 + (hh + 1) * 128], in_=qb[:, hh * 128:(hh + 1) * 128],
                    identity=ident), reads=[bqb] + RC, writes=[tbb])
        for ii, (qb, bqb, dstT, b_dst, _sc) in enumerate(items):
            src = tb[:, ii * 512:(ii + 1) * 512].rearrange("p (a b) -> p a b", b=128)
            if ii == 0 and len(items) == 2 and items[0][4]:
                add("act", lambda e, src=src, dstT=dstT: e.activation(out=dstT, in_=src, func=AF.Identity, scale=gqcol[:, 0:1]),
                    reads=RC, writes=[tbb, b_dst])
            elif ii == 0:
                add("act", lambda e, src=src, dstT=dstT: e.copy(out=dstT, in_=src), reads=[], writes=[tbb, b_dst])
            else:
                add("dve", lambda e, src=src, dstT=dstT: e.tensor_copy(out=dstT, in_=src), reads=[], writes=[tbb, b_dst])

    def front_pipeline(P, Q, W, jobs, FT_of, b_FT, b_QT, b_KT, group_end=None, extra_steps=None, bg_rate=3, after_loop=None, defer_extra=False):
        n = len(jobs)
        stt = [dict() for _ in range(n)]

        def stage_A(i):
            J, st = jobs[i], stt[i]
            xsl = P.actr % len(P.XS)
            P.actr += 1
            xt, bx, dx = P.XS[xsl], P.b_xs[xsl], P.d_xs[xsl]
            S.add("sp", lambda e, xt=xt, J=J: e.dma_start(out=xt, in_=J["x"]), writes=[bx], dsem=dx)
            c = P.ctr
            P.ctr += 1
            sst = P.st[c % 4]
            bst = P.b_st[c % 4]
            xn = P.XN[c % 2]
            bxn = P.b_xn[c % 2]
            add("act", lambda e: e.activation(out=junk, in_=xt, func=AF.Square, accum_out=sst[:, 0:1]),
                reads=[bx], writes=[bst])
            rstd_ops(sst[:, 0:1], sst[:, 1:2], 1.0 / 1024, bst)
            add("dve", lambda e: e.tensor_scalar(out=xn, in0=xt, scalar1=sst[:, 1:2], scalar2=None, op0=ALU.mult),
                reads=[bx, bst], writes=[bxn])
            st["xn"], st["bxn"] = xn, bxn

        def stage_B(i):
            J, st = jobs[i], stt[i]
            hslot = J["grp"] % 2
            t = J["t"]
            v = J["v"]
            xn, bxn = st["xn"], st["bxn"]
            pf, pb, bb, _bi = nb()
            for k in range(8):
                add("pe", lambda e, k=k: e.transpose(out=pb[:, k * 128:(k + 1) * 128], in_=xn[:, k * 128:(k + 1) * 128],
                                                     identity=ident), reads=[bxn] + RC, writes=[bb])
            ht = P.HT[hslot]
            bh0, bh1 = P.b_ht[hslot][t]
            htv = ht[:, :, t * 128:(t + 1) * 128]
            add("dve", lambda e: e.tensor_tensor(
                out=htv, in0=pb[:, 0:1024].rearrange("p (k c) -> p k c", c=128),
                in1=gmod[:, 0, :, v:v + 1].broadcast_to([128, 8, 128]), op=ALU.mult), reads=RM, writes=[bb, bh0])
            add("pool", lambda e: e.tensor_tensor(
                out=htv, in0=htv, in1=modcol[:, 0, :, v:v + 1].broadcast_to([128, 8, 128]), op=ALU.add),
                reads=RM + [bh0], writes=[bh0, bh1])

        def stage_C(i):
            J, st = jobs[i], stt[i]
            hslot = J["grp"] % 2
            t = J["t"]
            ht = P.HT[hslot]
            bh = P.b_ht[hslot][t]
            cgs = ([0] if J["need_q"] else []) + [1, 2]
            bk_ = {cg: nb() for cg in cgs}
            for k in range(8):
                for cg in cgs:
                    add("pe", lambda e, k=k, cg=cg, bkf=bk_[cg][0]: e.matmul(
                        bkf, lhsT=ht[:, k, t * 128:(t + 1) * 128], rhs=W.WIN[:, k, cg * 512:(cg + 1) * 512],
                        start=(k == 0), stop=(k == 7)), reads=[bh[0], bh[1], W.b_win[k]], writes=[bk_[cg][2]])
            st["banks"] = bk_
            for b_ in bk_.values():
                reserved.add(b_[3])
            if t == 3:
                FT, fcol0 = FT_of(J["grp"])
                allh = [b for tt in range(4) for b in P.b_ht[hslot][tt]]
                for g in range(4):
                    pf, pb, bb, _bi = nb()
                    for k in range(8):
                        add("pe", lambda e, g=g, k=k, pf=pf: e.matmul(
                            pf, lhsT=W.WIN[:, k, 1536 + g * 128: 1536 + (g + 1) * 128], rhs=ht[:, k, :],
                            start=(k == 0), stop=(k == 7)), reads=allh + [W.b_win[k]], writes=[bb])
                    if g % 2 == 0:
                        add("act", lambda e, g=g, pf=pf: e.copy(out=FT[:, g, fcol0:fcol0 + 512], in_=pf), reads=[], writes=[bb, b_FT])
                    else:
                        add("dve", lambda e, g=g, pf=pf: e.tensor_copy(out=FT[:, g, fcol0:fcol0 + 512], in_=pf), reads=[], writes=[bb, b_FT])

        def stage_D(i):
            J, st = jobs[i], stt[i]
            bk_ = st["banks"]
            items = []
            if J["need_q"]:
                qb, bqb = qk_D(Q, bk_[0], gqB, J["rope"])
                items.append((qb, bqb, J["qdst"], b_QT, J["rope"] is None))
            qb, bqb = qk_D(Q, bk_[1], gkB, J["rope"], kout=J["kout"])
            items.append((qb, bqb, J["kdst"], b_KT, False))
            st["items"] = items
            vf, _, vbb, _x = bk_[2]
            vx = J["vx"]
            add("act", lambda e, vf=vf, vx=vx: e.copy(out=vx[:, :, 0:128], in_=vf.rearrange("p (a b) -> p a b", b=128)),
                reads=[], writes=[vbb, J["b_vx"]])
            if J["vout"] is not None:
                vc = Q.vctr % len(Q.VO)
                Q.vctr += 1
                vo, bvo, dvo = Q.VO[vc], Q.b_vo[vc], Q.d_vo[vc]
                add("dve", lambda e, vf=vf, vo=vo: e.tensor_copy(out=vo, in_=vf), reads=[], writes=[vbb, bvo])
                S.add("sp", lambda e, vo=vo, J=J: e.dma_start(out=J["vout"], in_=vo), reads=[bvo], dsem=dvo)
            for b_ in bk_.values():
                reserved.discard(b_[3])

        bg = []

        def run_bg(nunits):
            while nunits > 0 and bg:
                try:
                    next(bg[0])
                    nunits -= 1
                except StopIteration:
                    bg.pop(0)

        for s_ in range(-2, n + 2):
            run_bg(bg_rate)
            if 0 <= s_ + 2 < n:
                stage_A(s_ + 2)
            if 0 <= s_ + 1 < n:
                stage_B(s_ + 1)
            if 0 <= s_ < n:
                stage_C(s_)
                stage_D(s_)
            if 0 <= s_ - 2 < n:
                qk_E(stt[s_ - 2]["items"])
            if 0 <= s_ < n and extra_steps and not defer_extra:
                extra_steps.pop(0)()
            if 0 <= s_ - 2 < n and group_end is not None:
                for pri, gen in group_end(jobs[s_ - 2]):
                    if pri:
                        bg.insert(0, gen)
                    else:
                        bg.append(gen)
        if after_loop is not None:
            while extra_steps:
                extra_steps.pop(0)()
                run_bg(1)
            after_loop()
        run_bg(1 << 30)

    def uv_tiles(FT, b_FT, fcol0, ntiles, UV, b_UV, uvt0):
        for t in range(ntiles):
            for gp in range(2):
                pf, pb, bb, _bi = nb()
                for gi in range(2):
                    g = gp * 2 + gi
                    add("pe", lambda e, g=g, gi=gi, t=t, pf=pf: e.matmul(
                        pf[:, gi * 256:(gi + 1) * 256], lhsT=FT[:, g, fcol0 + t * 128: fcol0 + (t + 1) * 128],
                        rhs=wcs[:, g, :], start=True, stop=True), reads=[b_FT, b_wcs], writes=[bb])
                eng = "dve" if (t + gp) % 2 == 0 else "act"
                dst = UV[:, uvt0 + t, gp * 2:gp * 2 + 2, :]
                if eng == "dve":
                    add("dve", lambda e, pf=pf, dst=dst: e.tensor_copy(out=dst, in_=pf.rearrange("p (a b) -> p a b", b=256)),
                        reads=[], writes=[bb, b_UV])
                else:
                    add("act", lambda e, pf=pf, dst=dst: e.copy(out=dst, in_=pf.rearrange("p (a b) -> p a b", b=256)),
                        reads=[], writes=[bb, b_UV])

    class Wt:
        pass

    def load_win(W):
        for k in range(8):
            S.add("pool", lambda e, k=k: e.dma_start(out=W.WIN[:, k, :], in_=w_in[k * 128:(k + 1) * 128, :]),
                  writes=[W.b_win[k]], dsem=W.d_win[k])

    PRE_S = {}

    def back_phase(xsrc_all, v, ydst, tagn, prefetch=False, win_prefetch=False, pf_w1=True):
        A.off = mod_mark
        mark = A.off
        XM0 = A.alloc([128, 4, 1024], F32)
        WOUT = A.alloc([128, 8, 1024], BF16)
        b_wo = S.buf("wout")
        d_wo = S.dsem(f"wo{tagn}")
        P = make_pipe(0, False, 1, 0)
        W2 = [A.alloc([128, 4, 1024], BF16) for _ in range(3)]
        b_w2 = S.bufs("w2s", 3)
        d_w2 = [S.dsem(f"w2{i}{tagn}") for i in range(3)]
        assert A.off <= WIN_S_OFF, (A.off, WIN_S_OFF)
        A.off = WIN_S_OFF
        W1 = [A.alloc([128, 8, 512], BF16) for _ in range(3)]
        b_w1 = S.bufs("w1s", 3)
        d_w1 = [S.dsem(f"w1{i}{tagn}") for i in range(3)]
        XNB = [A.alloc([128, 1024], BF16) for _ in range(4)]
        b_xnb = S.bufs("xnb", 4)
        assert A.off == WIN_S_OFF + 32768, A.off
        HID = A.alloc([128, 32, 512], BF16)
        b_hid = S.bufs("hid", 32, rw=False)
        XM1 = A.alloc([128, 4, 1024], F32)
        XMID2 = [XM0, XM1]
        b_xm2 = [S.bufs("xm", 4) for _ in range(2)]
        d_xm2 = [[S.dsem(f"xm{i}{g_}{tagn}") for i in range(4)] for g_ in range(2)]
        TM = [A.alloc([128, 512], F32) for _ in range(2)]
        b_tm = S.bufs("tm", 2)
        tmc = [0]
        RL = [A.alloc([128, 512], F32) for _ in range(3)]
        b_rl = S.bufs("rl", 3)
        rlc = [0]
        b_cat = S.buf("catr")
        w1c = [0]
        w2c = [0]
        snap_ = [None]

        def load_w1(jb):
            s = w1c[0] % 3
            w1c[0] += 1
            S.add("pool", lambda e: e.dma_start(out=W1[s], in_=w1[:, jb * 512:(jb + 1) * 512].rearrange("(k p) n -> p k n", p=128)),
                  writes=[b_w1[s]], dsem=d_w1[s], snap=snap_[0])
            return s

        def load_w2(jb):
            s = w2c[0] % 3
            w2c[0] += 1
            S.add("pool", lambda e: e.dma_start(out=W2[s], in_=w2[jb * 512:(jb + 1) * 512, :].rearrange("(j p) n -> p j n", p=128)),
                  writes=[b_w2[s]], dsem=d_w2[s])
            return s

        def evac_gate(pf, bb, gi, half, t, XMID, b_xm):
            i = tmc[0] % 2
            tmc[0] += 1
            tm, btm = TM[i], b_tm[i]
            add("dve", lambda e: e.tensor_tensor(out=tm, in0=pf, in1=gB[:, v, gi, half * 512:(half + 1) * 512], op=ALU.mult),
                reads=[b_gB], writes=[bb, btm])
            add("pool", lambda e: e.tensor_tensor(out=XMID[:, t, half * 512:(half + 1) * 512],
                                                  in0=XMID[:, t, half * 512:(half + 1) * 512], in1=tm, op=ALU.add),
                reads=[btm], writes=[b_xm[t]])

        ht = P.HT[0]
        allh = [b for t in range(4) for b in P.b_ht[0][t]]
        w1q = []
        w2q = []

        def xload(grp):
            r0 = grp * 512
            XMID, b_xm, d_xm = XMID2[grp], b_xm2[grp], d_xm2[grp]
            for t in range(4):
                S.add("sp", lambda e, t=t, r0=r0, XMID=XMID: e.dma_start(out=XMID[:, t, :], in_=xsrc_all[r0 + t * 128: r0 + (t + 1) * 128, :]),
                      writes=[b_xm[t]], dsem=d_xm[t], snap=snap_[0])

        def wout(grp):
            r0 = grp * 512
            XMID, b_xm = XMID2[grp], b_xm2[grp]
            S.mark(f'b{tagn}_wout{grp}')
            for t in range(4):
                b0, b1 = nb(), nb()
                for k in range(8):
                    for half, bk in ((0, b0), (1, b1)):
                        add("pe", lambda e, k=k, half=half, bk=bk, t=t, r0=r0: e.matmul(
                            bk[0], lhsT=CAT[:, k, r0 + t * 128: r0 + (t + 1) * 128], rhs=WOUT[:, k, half * 512:(half + 1) * 512],
                            start=(k == 0), stop=(k == 7)), reads=[b_cat, b_wo], writes=[bk[2]])
                evac_gate(b0[0], b0[2], 0, 0, t, XMID, b_xm)
                evac_gate(b1[0], b1[2], 0, 1, t, XMID, b_xm)

        def norm_part(grp):
            XMID, b_xm = XMID2[grp], b_xm2[grp]
            for t in range(4):
                c = P.ctr
                P.ctr += 1
                st = P.st[c % 4]
                bst = P.b_st[c % 4]
                xt = XMID[:, t, :]
                add("act", lambda e, xt=xt, st=st: e.activation(out=junk, in_=xt, func=AF.Square, accum_out=st[:, 0:1]),
                    reads=[b_xm[t]], writes=[bst])
                rstd_ops(st[:, 0:1], st[:, 1:2], 1.0 / 1024, bst)
                add("dve", lambda e, xt=xt, st=st, t=t: e.tensor_scalar(out=XNB[t], in0=xt, scalar1=st[:, 1:2], scalar2=None, op0=ALU.mult),
                    reads=[b_xm[t], bst], writes=[b_xnb[t]])

        def trans_part(grp):
            for t in range(4):
                pf, pb, bb, _bi = nb()
                for k in range(8):
                    add("pe", lambda e, k=k, t=t, pb=pb: e.transpose(out=pb[:, k * 128:(k + 1) * 128], in_=XNB[t][:, k * 128:(k + 1) * 128],
                                                                     identity=ident), reads=[b_xnb[t]] + RC, writes=[bb])
                for k in range(8):
                    par = k % 2
                    bh = P.b_ht[0][t][par]
                    if t % 2 == 0:
                        add("act", lambda e, k=k, t=t, pb=pb: e.activation(
                            out=ht[:, k, t * 128:(t + 1) * 128], in_=pb[:, k * 128:(k + 1) * 128], func=AF.Identity,
                            scale=gmod[:, 1, k, v:v + 1], bias=modcol[:, 2, k, v:v + 1]), reads=RM, writes=[bb, bh])
                    else:
                        add("dve", lambda e, k=k, t=t, pb=pb: e.tensor_scalar(
                            out=ht[:, k, t * 128:(t + 1) * 128], in0=pb[:, k * 128:(k + 1) * 128],
                            scalar1=gmod[:, 1, k, v:v + 1], scalar2=modcol[:, 2, k, v:v + 1], op0=ALU.mult, op1=ALU.add),
                            reads=RM, writes=[bb, bh])

        def w1_stage(grp, mid=None, mid2=None):
            S.mark(f'b{tagn}_w1_{grp}')
            for jb in range(8):
                if jb == 4 and mid is not None:
                    mid()
                if jb == 7 and mid2 is not None:
                    mid2()
                s_ = w1q.pop(0)
                if jb + 2 < 8:
                    w1q.append(load_w1(jb + 2))
                if jb == 3:
                    w2q.append(load_w2(0))
                if jb == 6:
                    w2q.append(load_w2(1))
                for jj in range(4):
                    pf, pb, bb, _bi = nb()
                    for k in range(8):
                        add("pe", lambda e, s_=s_, jj=jj, k=k, pf=pf: e.matmul(
                            pf, lhsT=W1[s_][:, k, jj * 128:(jj + 1) * 128], rhs=ht[:, k, :], start=(k == 0), stop=(k == 7)),
                            reads=allh + [b_w1[s_]], writes=[bb])
                    i = rlc[0] % 3
                    rlc[0] += 1
                    rl, brl = RL[i], b_rl[i]
                    add("act", lambda e, pf=pf, rl=rl: e.activation(out=rl, in_=pf, func=AF.Relu), reads=[], writes=[bb, brl])
                    j = jb * 4 + jj
                    eng = "pool" if j % 4 == 3 else "dve"
                    add(eng, lambda e, rl=rl, j=j: e.tensor_tensor(out=HID[:, j, :], in0=rl, in1=rl, op=ALU.mult),
                        reads=[brl], writes=[b_hid[j]])

        def w2_stage(grp, next_w1):
            r0 = grp * 512
            XMID, b_xm, d_xm = XMID2[grp], b_xm2[grp], d_xm2[grp]
            S.mark(f'b{tagn}_w2_{grp}')
            acc = [[nb(), nb()] for _ in range(4)]
            for jb in range(8):
                s_ = w2q.pop(0)
                if jb + 2 < 8:
                    w2q.append(load_w2(jb + 2))
                if next_w1 and jb in (5, 6):
                    w1q.append(load_w1(jb - 5))
                if (not next_w1) and win_prefetch:
                    kk = jb
                    S.add("pool", lambda e, kk=kk: e.dma_start(out=PRE_S["WIN"][:, kk, :], in_=w_in[kk * 128:(kk + 1) * 128, :]),
                          writes=list(b_w1) + list(b_xnb) + [PRE_S["b_win"][kk]], dsem=PRE_S["d_win"][kk])
                for jj in range(4):
                    j = jb * 4 + jj
                    for t in range(4):
                        for half in range(2):
                            bk = acc[t][half]
                            add("pe", lambda e, s_=s_, jj=jj, j=j, t=t, half=half, bk=bk: e.matmul(
                                bk[0], lhsT=HID[:, j, t * 128:(t + 1) * 128], rhs=W2[s_][:, jj, half * 512:(half + 1) * 512],
                                start=(j == 0), stop=(j == 31)), reads=[b_hid[j], b_w2[s_]], writes=[bk[2]])
            for t in range(4):
                for half in range(2):
                    evac_gate(acc[t][half][0], acc[t][half][2], 1, half, t, XMID, b_xm)
                S.add("sp", lambda e, t=t, r0=r0, XMID=XMID: e.dma_start(out=ydst[r0 + t * 128: r0 + (t + 1) * 128, :], in_=XMID[:, t, :]),
                      reads=[b_xm[t]], dsem=d_xm[t])

        if win_prefetch:
            wv = A.t[:, WIN_S_OFF:WIN_S_OFF + 32768].bitcast(BF16).rearrange("p (a b) -> p a b", b=2048)
            PRE_S["WIN"] = wv
            PRE_S["b_win"] = S.bufs("winS", 8)
            PRE_S["d_win"] = [S.dsem(f"winS{k}") for k in range(8)]
        if prefetch:
            snap_[0] = S.snapshot()
        S.add("pool", lambda e: e.dma_start(out=WOUT, in_=w_out.rearrange("(k p) n -> p k n", p=128)), writes=[b_wo], dsem=d_wo,
              snap=snap_[0])
        if pf_w1:
            w1q.append(load_w1(0))
            w1q.append(load_w1(1))
        xload(0)
        if prefetch:
            snap_[0] = None
            yield
        if not pf_w1:
            w1q.append(load_w1(0))
            w1q.append(load_w1(1))
        xload(1)
        wout(0)
        norm_part(0)
        trans_part(0)
        def mid1():
            wout(1)

        w1_stage(0, mid1, lambda: norm_part(1))
        trans_part(1)
        w2_stage(0, True)
        w1_stage(1)
        w2_stage(1, False)
        S.fence()
        A.off = mark
        yield

    def front_prompt(after_loop=None):
        mark = A.off
        W = Wt()
        W.WIN = A.alloc([128, 8, 2048], BF16)
        W.b_win = S.bufs("win", 8)
        W.d_win = [S.dsem(f"winp{k}") for k in range(8)]
        load_win(W)
        P = make_pipe(2, True)
        Q = make_qk(False, 2, 1)
        T = make_att(3, 4, 4, 256)
        T.acc_eng = 'act'
        QT = [A.alloc([128, 4, 512], BF16) for _ in range(2)]
        KT = [A.alloc([128, 4, 512], BF16) for _ in range(2)]
        VX = [A.alloc([128, 4, 4, 130], BF16) for _ in range(2)]
        FT1 = A.alloc([128, 4, 512], BF16)
        FT = [FT1, FT1]
        UV = A.alloc([128, 4, 4, 256], BF16)
        b_KT, b_FT, b_UV = S.buf("KT"), S.buf("FT"), S.buf("UV")
        b_VX = [S.bufs("VX", 4) for _ in range(2)]
        for gs in range(2):
            add("pool", lambda e, gs=gs: e.memset(VX[gs][:, :, :, 128:130], 1.0), reads=[], writes=b_VX[gs])
        jobs = []
        for grp in range(2):
            for t in range(4):
                r = grp * 512 + t * 128
                jobs.append(dict(x=xp[r:r + 128, :], v=0, need_q=True, rope=None,
                                 qdst=QT[grp][:, :, t * 128:(t + 1) * 128], kdst=KT[grp][:, :, t * 128:(t + 1) * 128],
                                 vx=VX[grp][:, t, :, :], b_vx=b_VX[grp][t], kout=nk[r:r + 128, :], vout=nv[r:r + 128, :],
                                 grp=grp, t=t))

        def batch_att(grp, bi):
            r0 = grp * 512
            bo = bi * 256
            ajobs = []
            for h in range(4):
                kts = [(KT[grp][:, h, bo + kt * 128: bo + (kt + 1) * 128], b_KT, VX[grp][:, bi * 2 + kt, h, 0:129],
                        b_VX[grp][bi * 2 + kt]) for kt in range(2)]
                ajobs.append(dict(q=QT[grp][:, h, bo:bo + 256], kts=kts, nq=256, col0=r0 + bo, h=h))
            for _ in attention_multi(T, ajobs):
                yield

        def group_dft(grp):
            r0 = grp * 512
            uv_tiles(FT[grp], b_FT, 0, 4, UV, b_UV, 0)
            yield
            for bi in range(2):
                bo = bi * 256
                if bi == 1:
                    yield
                for g in range(4):
                    pf, pb, bb, _bi = nb()
                    i = 0
                    for nt in range(2):
                        for cs in range(2):
                            add("pe", lambda e, g=g, nt=nt, cs=cs, pf=pf, i=i, bi=bi: e.matmul(
                                pf[:, 0:256], lhsT=UV[:, bi * 2 + nt, g, cs * 128:(cs + 1) * 128], rhs=dftp[:, nt, cs, :],
                                start=(i == 0), stop=(i == 3)), reads=[b_UV, b_dftp] + RC, writes=[bb])
                            i += 1
                    col = r0 + bo
                    if g % 2 == 0:
                        add("dve", lambda e, g=g, pf=pf, col=col: e.tensor_copy(out=CAT[:, 4 + g, col:col + 256], in_=pf[:, 0:256]),
                            reads=[], writes=[bb, T.b_cat])
                    else:
                        add("act", lambda e, g=g, pf=pf, col=col: e.copy(out=CAT[:, 4 + g, col:col + 256], in_=pf[:, 0:256]),
                            reads=[], writes=[bb, T.b_cat])

        def tile_end(J):
            out = []
            if J["t"] in (1, 3):
                out.append((False, batch_att(J["grp"], J["t"] // 2)))
            if J["t"] == 3:
                out.append((True, group_dft(J["grp"])))
            return out

        if os.environ.get('MK_DBG1'):
            while late_mod:
                late_mod.pop(0)()
        S.mark('fp_front')
        front_pipeline(P, Q, W, jobs, lambda grp: (FT[grp], 0), b_FT, T.b_q, b_KT, group_end=tile_end, extra_steps=late_mod, bg_rate=int(os.environ.get('MK_BGRATE', '1')), after_loop=after_loop, defer_extra=(after_loop is not None))
        while late_mod:
            late_mod.pop(0)()
        S.fence()
        A.off = mod_mark

    def front_sample(before_last_dft=None):
        mark = A.off
        QT = A.alloc([128, 4, 1024], BF16)
        KT = A.alloc([128, 4, 2560], BF16)
        VX = A.alloc([128, 20, 4, 130], BF16)
        FTA = A.alloc([128, 4, 2048], BF16)
        b_KT, b_FT = S.buf("KTs"), S.buf("FTs")
        b_VX = S.bufs("VXs", 20)
        W = Wt()
        win_off = A.off
        W.WIN = A.alloc([128, 8, 2048], BF16)
        assert win_off + (0 if win_off % 64 == 0 else 64 - win_off % 64) == WIN_S_OFF, (win_off, WIN_S_OFF)
        if "WIN" in PRE_S:
            W.b_win = PRE_S["b_win"]
            W.d_win = PRE_S["d_win"]
        else:
            W.b_win = S.bufs("win", 8)
            W.d_win = [S.dsem(f"wins{k}") for k in range(8)]
            load_win(W)
        pipe_off = A.off
        P = make_pipe(2, True)
        Q = make_qk()
        b_q = S.buf("qTs")
        for t0_ in range(0, 20, 4):
            add("pool", lambda e, t0_=t0_: e.memset(VX[:, t0_:t0_ + 4, :, 128:130], 1.0), reads=[], writes=b_VX[t0_:t0_ + 4])
        CKB = A.alloc([128, 4, 512], BF16)
        b_ckb = S.buf("ckb")
        d_ck = S.dsem("ck")
        d_cv = S.dsem("cv")
        S.add("pool", lambda e: e.dma_start(out=CKB, in_=ck.rearrange("(t p) n -> p t n", p=128)), writes=[b_ckb], dsem=d_ck)
        cvops = []
        for t in range(4):
            cvops.append(S.add("pool", lambda e, t=t: e.dma_start(out=VX[:, t, :, 0:128],
                                                                  in_=cv[t * 128:(t + 1) * 128, :].rearrange("p (h e) -> p h e", e=128)),
                               writes=[b_VX[t]], dsem=d_cv))
        for o in cvops:
            o.dcount = d_cv.count
        def ck_step(t):
            def run():
                tf, tb, tbb, _bi = nb()
                for hh in range(4):
                    add("pe", lambda e, hh=hh, t=t, tb=tb: e.transpose(out=tb[:, hh * 128:(hh + 1) * 128],
                                                                       in_=CKB[:, t, hh * 128:(hh + 1) * 128], identity=ident),
                        reads=[b_ckb] + RC, writes=[tbb])
                add("act", lambda e, t=t, tb=tb: e.copy(out=KT[:, :, t * 128:(t + 1) * 128],
                                                        in_=tb[:, 0:512].rearrange("p (a b) -> p a b", b=128)),
                    reads=[], writes=[tbb, b_KT])
            return run
        ck_steps = [lambda: None, lambda: None] + [ck_step(t) for t in range(4)]
        jobs = []
        for grp in range(4):
            own = grp < 2
            for t in range(4):
                r = grp * 512 + t * 128
                jobs.append(dict(x=xs[r:r + 128, :], v=1, need_q=own, rope=grp * 4 + t,
                                 qdst=QT[:, :, r:r + 128] if own else None, kdst=KT[:, :, 512 + r: 512 + r + 128],
                                 vx=VX[:, 4 + grp * 4 + t, :, :], b_vx=b_VX[4 + grp * 4 + t], kout=None, vout=None,
                                 grp=grp, t=t))
        S.mark('fs_front')
        front_pipeline(P, Q, W, jobs, lambda grp: (FTA, grp * 512), b_FT, b_q, b_KT, extra_steps=ck_steps)
        while ck_steps:
            ck_steps.pop(0)()
        S.fence()
        A.off = win_off
        UV = A.alloc([128, 16, 4, 256], BF16)
        b_UV = S.buf("UVs")
        assert A.off <= pipe_off
        A.off = pipe_off
        T = make_att(6, 8, 8)
        T.b_q = b_q
        NDS = 8
        DS = [A.alloc([128, 2, 512], BF16) for _ in range(NDS)]
        b_ds = S.bufs("ds", NDS)
        d_ds = [S.dsem(f"ds{i}") for i in range(NDS)]
        S.mark('fs_uv')
        uv_tiles(FTA, b_FT, 0, 16, UV, b_UV, 0)
        dsc = [0]

        def dft_chunk(nc_):
            accb = [nb() for _ in range(4)]
            for nt in range(16):
                s = dsc[0] % NDS
                dsc[0] += 1
                S.add("sp", lambda e, s=s, nt=nt: e.dma_start(
                    out=DS[s], in_=dfts_d[nt * 128:(nt + 1) * 128, :, nc_ * 512:(nc_ + 1) * 512]),
                    writes=[b_ds[s]], dsem=d_ds[s])
                for g in range(4):
                    for cs in range(2):
                        add("pe", lambda e, s=s, nt=nt, g=g, cs=cs: e.matmul(
                            accb[g][0], lhsT=UV[:, nt, g, cs * 128:(cs + 1) * 128], rhs=DS[s][:, cs, :],
                            start=(nt == 0 and cs == 0), stop=(nt == 15 and cs == 1)), reads=[b_UV, b_ds[s]], writes=[accb[g][2]])
            for g in range(4):
                pf = accb[g][0]
                col = nc_ * 512
                if g % 2 == 0:
                    add("dve", lambda e, g=g, pf=pf, col=col: e.tensor_copy(out=CAT[:, 4 + g, col:col + 512], in_=pf),
                        reads=[], writes=[accb[g][2], T.b_cat])
                else:
                    add("act", lambda e, g=g, pf=pf, col=col: e.copy(out=CAT[:, 4 + g, col:col + 512], in_=pf),
                        reads=[], writes=[accb[g][2], T.b_cat])

        for qc in range(2):
            S.mark(f'fs_att{qc}')
            ajobs = []
            for h in range(4):
                kts = [(KT[:, h, kt * 128:(kt + 1) * 128], b_KT, VX[:, kt, h, 0:129], b_VX[kt]) for kt in range(20)]
                ajobs.append(dict(q=QT[:, h, qc * 512:(qc + 1) * 512], kts=kts, nq=512, col0=qc * 512, h=h))
            for _ in attention_multi(T, ajobs):
                pass
            S.mark(f'fs_dft{qc}')
            if qc == 1 and before_last_dft is not None:
                before_last_dft()
            dft_chunk(qc)
        S.fence()
        A.off = mark

    phases = debug_phases or ("fp", "bp", "fs", "bs")
    BP = [None]
    if "fp" in phases:
        if "bp" in phases:
            def hook():
                BP[0] = back_phase(xp, 0, yp, "p", prefetch=True, win_prefetch=("fs" in phases))
                next(BP[0])
            front_prompt(hook)
        else:
            front_prompt()
    if "bp" in phases:
        if BP[0] is None:
            BP[0] = back_phase(xp, 0, yp, "p", prefetch=False, win_prefetch=("fs" in phases))
        for _ in BP[0]:
            pass
    BS = [None]
    if "fs" in phases:
        if "bs" in phases:
            def hook2():
                BS[0] = back_phase(xs, 1, ys, "s", prefetch=True, pf_w1=False)
                next(BS[0])
            front_sample(hook2)
        else:
            front_sample()
    if "bs" in phases:
        if BS[0] is None:
            BS[0] = back_phase(xs, 1, ys, "s")
        for _ in BS[0]:
            pass
    S.mark('end')
    S.emit()
    import json as _json
    if os.environ.get('MK_MARKS'):
        _json.dump(S.marks, open(os.environ['MK_MARKS'], 'w'))
        _json.dump(S.waitlog, open(os.environ['MK_MARKS'] + '.waits', 'w'))
    print("arena peak bytes", A.peak, "ops", len(S.ops), {e: len(S.eng_ops[e]) for e in QUEUES})
    return nc


def _consts(half):
    bf = ml_dtypes.bfloat16
    ident = np.eye(128, dtype=np.float32).astype(bf)
    c = np.arange(128)
    ang = 2 * np.pi * ((c[:, None] * c[None, :]) % 128) / 128.0
    ccs = np.concatenate([np.cos(ang), -np.sin(ang)], axis=1).astype(np.float32).astype(bf)
    n = np.arange(256)
    ang = 2 * np.pi * ((n[:, None] * n[None, :]) % 256) / 256.0
    sc = 1.0 / np.sqrt(256.0 * 128.0)
    dftp = np.stack([np.cos(ang) * sc, np.sin(ang) * sc], axis=1).astype(np.float32).astype(bf)
    pos = np.concatenate([half * 1024 + np.arange(1024), (1 - half) * 1024 + np.arange(1024)])
    npos = half * 1024 + np.arange(1024)
    ang = 2 * np.pi * ((pos[:, None].astype(np.int64) * npos[None, :]) % 2048) / 2048.0
    sc = 1.0 / np.sqrt(2048.0 * 128.0)
    dfts = np.stack([np.cos(ang) * sc, np.sin(ang) * sc], axis=1).astype(np.float32).astype(bf)
    inv = (np.float32(10000.0) ** (-np.arange(0, 32, 2, dtype=np.float32) / np.float32(32))).astype(np.float32)
    row = (pos // 64).astype(np.float32)
    col = (pos % 64).astype(np.float32)
    ar = row[:, None] * inv[None, :]
    ac = col[:, None] * inv[None, :]
    cos = np.concatenate([np.cos(ar), np.cos(ac)], axis=1)
    sin = np.concatenate([np.sin(ar), np.sin(ac)], axis=1)
    rope = np.stack([cos, sin], axis=1).astype(np.float32)
    dftp = np.ascontiguousarray(dftp.reshape(2, 128, 2, 256).transpose(1, 0, 2, 3).reshape(128, 1024))
    rope = np.ascontiguousarray(rope.reshape(16, 128, 2, 32).transpose(1, 0, 2, 3).reshape(128, 1024))
    return dict(ident=ident, ccs=ccs, dftp=dftp, dfts=dfts, rope=rope, ident32=np.eye(64, dtype=np.float32))


_NC_CACHE = {}


def kernel(x_prompt, x_sample, c, cache_k, cache_v, c_ctx, w_mod, b_mod, norm1_g, w_in,
           q_norm_g, k_norm_g, lambda_q1, lambda_k1, lambda_q2, lambda_k2, subln_g,
           w_four, w_out, norm2_g, w1, w2):
    f = lambda a: np.ascontiguousarray(np.asarray(a, dtype=np.float32))
    x_prompt, x_sample, c, cache_k, cache_v, c_ctx = map(f, (x_prompt, x_sample, c, cache_k, cache_v, c_ctx))
    phases = os.environ.get("MK_PHASES")
    phases = tuple(phases.split(",")) if phases else None
    key = phases
    if key not in _NC_CACHE:
        _NC_CACHE[key] = build_nc(phases)
    nc = _NC_CACHE[key]
    shared = dict(
        w_mod=f(w_mod)[0], b_mod=f(b_mod)[0], norm1_g=f(norm1_g)[0], w_in=f(w_in)[0], qg=f(q_norm_g)[0], kg=f(k_norm_g)[0],
        lam4=np.concatenate([f(lambda_q1)[0], f(lambda_k1)[0], f(lambda_q2)[0], f(lambda_k2)[0]]),
        subg=f(subln_g)[0], w_four=f(w_four)[0], w_out=f(w_out)[0], norm2_g=f(norm2_g)[0], w1=f(w1)[0], w2=f(w2)[0])
    cons = [_consts(0), _consts(1)]
    xpf = x_prompt.reshape(8192, 1024)
    in_maps = []
    for i in range(8):
        b, half = i // 2, i % 2
        own = x_sample[b, half * 1024:(half + 1) * 1024]
        oth = x_sample[b, (1 - half) * 1024:(2 - half) * 1024]
        m = dict(shared)
        m.update(cons[half])
        m.update(
            xp=np.ascontiguousarray(xpf[i * 1024:(i + 1) * 1024]),
            xs=np.ascontiguousarray(np.concatenate([own, oth], axis=0)),
            cvec=np.ascontiguousarray(np.stack([c_ctx, c[b]], axis=0)),
            ck=np.ascontiguousarray(cache_k[b, 0].reshape(512, 512)),
            cv=np.ascontiguousarray(cache_v[b, 0].reshape(512, 512)),
        )
        in_maps.append(m)
    res = run_bass_kernel_spmd(nc, in_maps, core_ids=list(range(8)))
    R = res.results
    y_prompt = np.concatenate([R[i]["yp"] for i in range(8)], axis=0).reshape(32, 256, 1024)
    y_sample = np.concatenate([R[i]["ys"] for i in range(8)], axis=0).reshape(4, 2048, 1024)
    nkk = np.concatenate([R[i]["nk"] for i in range(8)], axis=0).reshape(32, 1, 256, 4, 2, 64)
    nvv = np.concatenate([R[i]["nv"] for i in range(8)], axis=0).reshape(32, 1, 256, 4, 128)
    return (y_prompt.astype(np.float32), y_sample.astype(np.float32), nkk.astype(np.float32), nvv.astype(np.float32))
```

```python
import os
import numpy as np
import ml_dtypes
import concourse.bass as bass
import concourse.mybir as mybir
from concourse.bass_utils import run_bass_kernel_spmd

F32 = mybir.dt.float32
BF16 = mybir.dt.bfloat16
U8 = mybir.dt.uint8
AF = mybir.ActivationFunctionType
ALU = mybir.AluOpType
AX = mybir.AxisListType
EPS = 1e-6
LAM_INIT = 0.8 - 0.6 * 1.0


class Buf:
    __slots__ = ("name", "last_w", "readers", "rw")

    def __init__(self, name, rw=True):
        self.name = name
        self.last_w = None
        self.readers = []
        self.rw = rw


class DmaSem:
    __slots__ = ("sem", "count", "name")

    def __init__(self, name):
        self.name = name
        self.sem = None
        self.count = 0


class Op:
    __slots__ = ("eng", "fn", "idx", "eidx", "deps", "signal", "count", "dsem", "dcount", "is_dma", "fence", "raw", "snap")

    def __init__(self, eng, fn, idx):
        self.eng = eng
        self.fn = fn
        self.idx = idx
        self.eidx = 0
        self.deps = {}
        self.signal = False
        self.count = 0
        self.dsem = None
        self.dcount = 0
        self.is_dma = False
        self.fence = None
        self.snap = None


QUEUES = ("pe", "act", "dve", "pool", "sp")


class Sched:
    def __init__(self, nc):
        self.nc = nc
        self.ops = []
        self.eng_ops = {e: [] for e in QUEUES}
        self.dsems = []
        self.cur_fence = None
        self.marks = []
        self.waitlog = {}

    def mark(self, name):
        self.marks.append((name, len(self.eng_ops["pe"])))

    def buf(self, name, rw=True):
        return Buf(name, rw)

    def bufs(self, name, n, rw=True):
        return [Buf(f"{name}{i}", rw) for i in range(n)]

    def dsem(self, name):
        d = DmaSem(name)
        self.dsems.append(d)
        return d

    def fence(self):
        need = {}
        for e in QUEUES:
            for op in reversed(self.eng_ops[e]):
                if not op.is_dma:
                    need[e] = op
                    op.signal = True
                    break
        dneed = {d: d.count for d in self.dsems if d.count}
        self.cur_fence = (need, dneed)

    def snapshot(self):
        need = {}
        for e in QUEUES:
            for op in reversed(self.eng_ops[e]):
                if not op.is_dma:
                    need[e] = op
                    op.signal = True
                    break
        dneed = {d: d.count for d in self.dsems if d.count}
        return (need, dneed)

    def add(self, eng, fn, reads=(), writes=(), dsem=None, snap=None):
        op = Op(eng, fn, len(self.ops))
        op.snap = snap
        op.eidx = len(self.eng_ops[eng])
        op.fence = self.cur_fence
        if dsem is not None:
            op.is_dma = True
            op.dsem = dsem
            dsem.count += 16
            op.dcount = dsem.count
        deps = {}
        raw = set()
        for b in reads:
            o = b.last_w
            if o is not None:
                deps[o.idx] = o
                raw.add(o.idx)
        for b in writes:
            o = b.last_w
            if o is not None:
                deps[o.idx] = o
                if b.rw:
                    raw.add(o.idx)
            for r in b.readers:
                deps[r.idx] = r
        op.raw = raw
        for b in writes:
            b.last_w = op
            b.readers = []
        for b in reads:
            b.readers.append(op)
        deps.pop(op.idx, None)
        op.deps = deps
        self.ops.append(op)
        self.eng_ops[eng].append(op)
        return op

    def finalize(self):
        for op in self.ops:
            need = {}
            dneed = {}
            for fz in (op.fence, op.snap):
                if fz is None:
                    continue
                fn_, fd_ = fz
                for e, d in fn_.items():
                    if e == op.eng and not op.is_dma and (op.eng == "pe" or op.eidx - d.eidx > 4):
                        continue
                    if e not in need or need[e].eidx < d.eidx:
                        need[e] = d
                for k_, c_ in fd_.items():
                    if k_ not in dneed or dneed[k_] < c_:
                        dneed[k_] = c_
            for d in op.deps.values():
                if d.is_dma:
                    k = d.dsem
                    if k not in dneed or dneed[k] < d.dcount:
                        dneed[k] = d.dcount
                else:
                    if d.eng == op.eng and not op.is_dma:
                        if op.eng == "pe":
                            continue
                        if op.eidx - d.eidx > 4:
                            continue
                        if d.idx not in op.raw:
                            continue
                    if d.eng not in need or need[d.eng].eidx < d.eidx:
                        need[d.eng] = d
            op.deps = (need, dneed)
            for d in need.values():
                d.signal = True
        for e in QUEUES:
            c = 0
            for op in self.eng_ops[e]:
                if op.signal and not op.is_dma:
                    c += 1
                    op.count = c

    def emit(self):
        import contextlib
        nc = self.nc
        self.finalize()
        with contextlib.ExitStack() as st:
            esem = {e: st.enter_context(nc.semaphore(f"s_{e}")) for e in QUEUES}
            for d in self.dsems:
                d.sem = st.enter_context(nc.semaphore(f"d_{d.name}"))
            block = st.enter_context(nc.Block())

            def run(eng_name):
                def body(eng):
                    waited = {}
                    wl = self.waitlog.setdefault(eng_name, [])
                    for op in self.eng_ops[eng_name]:
                        need, dneed = op.deps
                        for e, d in need.items():
                            if waited.get(e, 0) < d.count:
                                eng.wait_ge(esem[e], d.count)
                                waited[e] = d.count
                                wl.append((e, d.fn.__code__.co_firstlineno, op.fn.__code__.co_firstlineno))
                        for ds, cnt in dneed.items():
                            if waited.get(ds, 0) < cnt:
                                eng.wait_ge(ds.sem, cnt)
                                waited[ds] = cnt
                                wl.append(("dma:" + ds.name, 0, op.fn.__code__.co_firstlineno))
                        ins = op.fn(eng)
                        if op.is_dma:
                            ins.then_inc(op.dsem.sem, 16)
                        elif op.signal:
                            ins.then_inc(esem[eng_name], 1)
                    if eng_name == "sp":
                        for ds in self.dsems:
                            if ds.count and waited.get(ds, 0) < ds.count:
                                eng.wait_ge(ds.sem, ds.count)
                return body

            block.tensor(run("pe"))
            block.scalar(run("act"))
            block.vector(run("dve"))
            block.gpsimd(run("pool"))
            block.sync(run("sp"))


class Arena:
    def __init__(self, nc, nbytes):
        self.t = nc.alloc_sbuf_tensor("arena", [128, nbytes], U8)
        self.nbytes = nbytes
        self.off = 0
        self.peak = 0

    def alloc(self, shape, dt):
        esz = 4 if dt == F32 else 2
        nb = int(np.prod(shape[1:])) * esz
        off = (self.off + 63) // 64 * 64
        assert off + nb <= self.nbytes, f"arena overflow {off + nb} > {self.nbytes}"
        self.off = off + nb
        self.peak = max(self.peak, self.off)
        if os.environ.get('MK_ALLOCLOG'):
            import traceback
            fr = traceback.extract_stack(limit=3)[0]
            print(f"ALLOC off={off:7d} nb={nb:6d} end={off+nb:7d} line={fr.lineno} {fr.line[:50]}")
        v = self.t[:, off:off + nb].bitcast(dt)
        if len(shape) == 3:
            v = v.rearrange("p (a b) -> p a b", b=shape[2])
        elif len(shape) == 4:
            v = v.rearrange("p (a b c) -> p a b c", b=shape[2], c=shape[3])
        return v


def build_nc(debug_phases=None):
    nc = bass.Bass("TRN2", target_bir_lowering=False)

    def din(name, shape, d=F32):
        return nc.dram_tensor(name, list(shape), d, kind="ExternalInput").ap()

    def dout(name, shape):
        return nc.dram_tensor(name, list(shape), F32, kind="ExternalOutput").ap()

    xp = din("xp", [1024, 1024])
    xs = din("xs", [2048, 1024])
    cvec = din("cvec", [2, 1024])
    ck = din("ck", [512, 512])
    cv = din("cv", [512, 512])
    w_mod = din("w_mod", [1024, 6144])
    b_mod = din("b_mod", [6144])
    norm1_g = din("norm1_g", [1024])
    w_in = din("w_in", [1024, 2048])
    qg = din("qg", [64])
    kg = din("kg", [64])
    lam4 = din("lam4", [256])
    subg = din("subg", [128])
    w_four = din("w_four", [4, 128, 128])
    w_out = din("w_out", [1024, 1024])
    norm2_g = din("norm2_g", [1024])
    w1 = din("w1", [1024, 4096])
    w2 = din("w2", [4096, 1024])
    ident_d = din("ident", [128, 128], BF16)
    ccs_d = din("ccs", [128, 256], BF16)
    dftp_d = din("dftp", [128, 1024], BF16)
    dfts_d = din("dfts", [2048, 2, 1024], BF16)
    rope_d = din("rope", [128, 1024])
    ident32_d = din("ident32", [64, 64])
    yp = dout("yp", [1024, 1024])
    ys = dout("ys", [1024, 1024])
    nk = dout("nk", [1024, 512])
    nv = dout("nv", [1024, 512])

    S = Sched(nc)
    A = Arena(nc, 212480)

    banks = []
    for i in range(8):
        t = nc.alloc_psum_tensor(f"bank{i}", [128, 512], F32)
        banks.append((t[:], t[:].bitcast(BF16), S.buf(f"bank{i}", rw=False)))
    bank_ctr = [0]
    reserved = set()
    alloc_stamp = [-1.0] * 8

    def nb():
        assert len(reserved) < 8, "no free PSUM bank"
        best, bestk = None, None
        for i in range(8):
            if i in reserved:
                continue
            lw = banks[i][2].last_w
            k = max(-1.0 if lw is None else float(lw.idx), alloc_stamp[i])
            if bestk is None or k < bestk:
                best, bestk = i, k
        bank_ctr[0] += 1
        alloc_stamp[best] = len(S.ops) + bank_ctr[0] * 1e-7
        return banks[best] + (best,)

    uid = [0]

    def uname(p):
        uid[0] += 1
        return f"{p}{uid[0]}"

    ident = A.alloc([128, 128], BF16)
    ccs = A.alloc([128, 256], BF16)
    wfour = A.alloc([128, 4, 128], BF16)
    wcs = A.alloc([128, 4, 256], BF16)
    dftp = A.alloc([128, 2, 2, 256], BF16)
    rope = A.alloc([128, 16, 2, 32], F32)
    gqB = A.alloc([128, 64], F32)
    gkB = A.alloc([128, 64], F32)
    gqcol = A.alloc([128, 2], F32)
    subgB = A.alloc([128, 128], F32)
    lamB = A.alloc([128, 4, 64], F32)
    lamt = A.alloc([128, 2, 64], F32)
    lams = A.alloc([128, 8], F32)
    colv = A.alloc([128, 64], F32)
    V64 = A.alloc([128, 128], F32)
    ident32 = A.alloc([128, 64], F32)
    cT = colv[:, 0:16].rearrange("p (v k) -> p v k", k=8)
    sil = A.alloc([128, 2, 8], F32)
    silb = A.alloc([128, 8, 2], BF16)
    screp = A.alloc([128, 2, 8, 128], BF16)
    ncol = colv[:, 16:32].rearrange("p (v k) -> p v k", k=8)
    bmcol = colv[:, 32:64].rearrange("p (v k) -> p v k", k=8)
    modcol = A.alloc([128, 4, 8, 2], F32)
    gmod = A.alloc([128, 2, 8, 2], F32)
    gB = A.alloc([128, 2, 2, 1024], F32)
    stat = A.alloc([128, 64], F32)
    junk = A.alloc([128, 1024], BF16)
    epsT = A.alloc([128, 8], F32)
    CAT = A.alloc([128, 8, 1024], BF16)
    cat_off = A.off - 8 * 1024 * 2
    if os.environ.get('MK_MARKS'):
        print('CAT_OFF', cat_off)
    phase_base = A.off

    dconst = S.dsem("const")
    b_const = S.buf("const")
    const_ops = []

    dcrit = S.dsem("crit")
    b_crit = S.buf("crit")
    crit_ops = []

    def cload(eng, out, in_, crit=False, **kw):
        if crit:
            crit_ops.append(S.add(eng, lambda e: e.dma_start(out=out, in_=in_, **kw), writes=[], dsem=dcrit))
        else:
            const_ops.append(S.add(eng, lambda e: e.dma_start(out=out, in_=in_, **kw), writes=[], dsem=dconst))

    cload("sp", V64[0:16, :], cvec.rearrange("v (k c) -> (v k) c", c=128), crit=True)
    cload("sp", ident32[0:64, :], ident32_d, crit=True)
    cload("sp", V64[16:24, :], norm1_g.rearrange("(k c) -> k c", c=128), crit=True)
    cload("sp", V64[24:32, :], norm2_g.rearrange("(k c) -> k c", c=128), crit=True)
    for jj, j in enumerate((0, 1, 3, 4)):
        cload("sp", V64[32 + jj * 8: 40 + jj * 8, :], b_mod[j * 1024:(j + 1) * 1024].rearrange("(k c) -> k c", c=128), crit=True)
    cload("sp", ident, ident_d)
    cload("sp", ccs, ccs_d)
    d_tab = S.dsem("tabs")
    b_dftp, b_rope = S.buf("dftp"), S.buf("rope")
    cload("sp", gqB, qg.partition_broadcast(128))
    cload("sp", gkB, kg.partition_broadcast(128))
    cload("sp", gqcol[0:64, 0:1], qg.rearrange("(p o) -> p o", o=1))
    cload("sp", gqcol[64:128, 0:1], qg.rearrange("(p o) -> p o", o=1))
    cload("sp", subgB, subg.partition_broadcast(128))
    cload("sp", lamB.rearrange("p a b -> p (a b)"), lam4.partition_broadcast(128))
    d_wf = S.dsem("wfour")
    b_wf = S.buf("wfour")
    S.add("pool", lambda e: e.dma_start(out=wfour, in_=w_four.rearrange("g c e -> c g e")), writes=[b_wf], dsem=d_wf)
    for o in const_ops:
        o.dcount = dconst.count
    b_const.last_w = const_ops[-1]
    _t1 = S.add("sp", lambda e: e.dma_start(out=dftp, in_=dftp_d.rearrange("p (t c n) -> p t c n", t=2, c=2)), writes=[b_dftp], dsem=d_tab)
    _t2 = S.add("sp", lambda e: e.dma_start(out=rope, in_=rope_d.rearrange("p (t c f) -> p t c f", t=16, c=2)), writes=[b_rope], dsem=d_tab)
    _t1.dcount = d_tab.count
    _t2.dcount = d_tab.count
    b_colv = S.buf("colv")
    _pf, _pb, _bb, _bi = nb()
    for o in crit_ops:
        o.dcount = dcrit.count
    b_crit.last_w = crit_ops[-1]
    S.add("pe", lambda e: e.transpose(out=_pf[:, 0:64], in_=V64[0:64, :], identity=ident32[0:64, 0:64]),
          reads=[b_crit], writes=[_bb])
    S.add("dve", lambda e: e.tensor_copy(out=colv, in_=_pf[:, 0:64]), reads=[], writes=[_bb, b_colv])
    RC = [b_const, b_colv]

    def add(eng, fn, reads=(), writes=()):
        return S.add(eng, fn, reads=list(reads), writes=list(writes))

    b_lam = S.buf("lam")
    add("dve", lambda e: e.tensor_tensor(out=lamt, in0=lamB[:, 0::2, :], in1=lamB[:, 1::2, :], op=ALU.mult),
        reads=RC, writes=[b_lam])
    add("dve", lambda e: e.tensor_reduce(out=lams[:, 0:2], in_=lamt, axis=AX.X, op=ALU.add), reads=[b_lam], writes=[b_lam])
    add("act", lambda e: e.activation(out=lams[:, 0:2], in_=lams[:, 0:2], func=AF.Exp), reads=[b_lam], writes=[b_lam])
    add("dve", lambda e: e.tensor_tensor(out=lams[:, 2:3], in0=lams[:, 0:1], in1=lams[:, 1:2], op=ALU.subtract),
        reads=[b_lam], writes=[b_lam])
    add("dve", lambda e: e.tensor_scalar(out=lams[:, 3:4], in0=lams[:, 2:3], scalar1=LAM_INIT, scalar2=-1.0,
                                         op0=ALU.add, op1=ALU.mult), reads=[b_lam], writes=[b_lam])
    NLAM = lams[:, 3:4]
    b_eps = S.buf('eps')
    add('dve', lambda e: e.memset(epsT, EPS), reads=[], writes=[b_eps])
    b_subg = S.buf("subg")
    add("dve", lambda e: e.tensor_scalar(out=subgB, in0=subgB, scalar1=1.0 - LAM_INIT, scalar2=None, op0=ALU.mult),
        reads=RC, writes=[b_subg])
    b_wcs = S.buf("wcs")
    for gp in range(2):
        pf, pb, bb, _bi = nb()
        for gi in range(2):
            g = gp * 2 + gi
            for cs in range(2):
                add("pe", lambda e, g=g, gi=gi, cs=cs, pf=pf: e.matmul(
                    pf[:, gi * 256 + cs * 128: gi * 256 + (cs + 1) * 128], lhsT=ccs[:, cs * 128:(cs + 1) * 128],
                    rhs=wfour[:, g, :], start=True, stop=True), reads=RC + [b_wf], writes=[bb])
        add("dve", lambda e, gp=gp, pf=pf: e.tensor_copy(
            out=wcs[:, gp * 2:gp * 2 + 2, :], in_=pf.rearrange("p (a b) -> p a b", b=256)),
            reads=[], writes=[bb, b_wcs])

    b_sil = S.buf("sil")
    add("act", lambda e: e.activation(out=sil, in_=cT, func=AF.Silu), reads=[b_colv], writes=[b_sil])
    add("dve", lambda e: e.tensor_copy(out=silb, in_=sil.rearrange("p v k -> p k v")), reads=[b_sil], writes=[b_sil])
    add("dve", lambda e: e.tensor_copy(out=screp.rearrange("p v k m -> p (v k) m"),
                                       in_=sil.rearrange("p v k -> p (v k)").unsqueeze(2).broadcast_to([128, 16, 128])),
        reads=[b_sil], writes=[b_sil])

    mod_mark = A.off
    WIN_S_OFF = mod_mark + 8192 + 20480 + 20800 + 16384
    WM = [A.alloc([128, 8, 512], BF16) for _ in range(2)]
    bgt = A.alloc([128, 512], F32)
    b_wm = S.bufs("wm", 2)
    d_wm = [S.dsem(f"wm{i}") for i in range(2)]
    d_bg = S.dsem("bg")
    b_bg = S.buf("bg")
    wm_ctr = [0]
    b_mod_ = S.buf("modcol")
    b_gB = S.buf("gB")

    def load_wm(col0):
        s_ = wm_ctr[0] % 2
        wm_ctr[0] += 1
        S.add("pool", lambda e: e.dma_start(out=WM[s_], in_=w_mod[:, col0:col0 + 512].rearrange("(k p) n -> p k n", p=128)),
              writes=[b_wm[s_]], dsem=d_wm[s_])
        return s_

    def mod_vec_steps(jj, j):
        hold = {}

        def mk(half):
            def load():
                hold[("s", half)] = load_wm(j * 1024 + half * 512)

            def comp():
                if half == 0:
                    hold["b"] = nb()
                    reserved.add(hold["b"][3])
                pf, pb, bb, _bi = hold["b"]
                s_ = hold[("s", half)]
                for fc in range(4):
                    fcg = half * 4 + fc
                    for k in range(8):
                        add("pe", lambda e, s_=s_, fc=fc, fcg=fcg, k=k, pf=pf: e.matmul(
                            pf[:, fcg * 2:fcg * 2 + 2], lhsT=WM[s_][:, k, fc * 128:(fc + 1) * 128], rhs=silb[:, k, :],
                            start=(k == 0), stop=(k == 7)), reads=[b_wm[s_], b_sil], writes=[bb])
                if half == 1:
                    add("dve", lambda e, pf=pf: e.tensor_tensor(
                        out=modcol[:, jj, :, :], in0=pf[:, 0:16].rearrange("p (a b) -> p a b", b=2),
                        in1=bmcol[:, jj, :].unsqueeze(2).broadcast_to([128, 8, 2]), op=ALU.add), reads=RC, writes=[bb, b_mod_])
                    reserved.discard(hold["b"][3])
            return (load, comp)
        return [mk(0), mk(1)]

    def mod_gate_steps(gi, j):
        hold = {}

        def mk(half):
            def load():
                hold[half] = load_wm(j * 1024 + half * 512)

            def comp():
                s_ = hold[half]
                S.add("sp", lambda e: e.dma_start(
                    out=bgt, in_=b_mod[j * 1024 + half * 512: j * 1024 + (half + 1) * 512].partition_broadcast(128)),
                    writes=[b_bg], dsem=d_bg)
                for v in range(2):
                    pf, pb, bb, _bi = nb()
                    for k in range(8):
                        add("pe", lambda e, s_=s_, v=v, k=k, pf=pf: e.matmul(
                            pf, lhsT=screp[:, v, k, :], rhs=WM[s_][:, k, :], start=(k == 0), stop=(k == 7)),
                            reads=[b_wm[s_], b_sil], writes=[bb])
                    add("dve", lambda e, v=v, pf=pf: e.tensor_tensor(
                        out=gB[:, v, gi, half * 512:(half + 1) * 512], in0=pf, in1=bgt, op=ALU.add),
                        reads=[b_bg], writes=[bb, b_gB])
            return (load, comp)
        return [mk(0), mk(1)]

    def mod_finish(n):
        add("dve", lambda e: e.scalar_tensor_tensor(
            out=gmod[:, n, :, :], in0=modcol[:, 2 * n + 1, :, :], scalar=1.0,
            in1=ncol[:, n, :].unsqueeze(2).broadcast_to([128, 8, 2]), op0=ALU.add, op1=ALU.mult),
            reads=RC + [b_mod_], writes=[b_mod_])

    S.mark('mod')
    v34 = mod_vec_steps(3, 4)
    v23 = mod_vec_steps(2, 3)
    msteps = mod_vec_steps(1, 1) + mod_vec_steps(0, 0) + [(lambda: None, lambda: mod_finish(0))] + \
        mod_gate_steps(0, 2) + v34 + v23 + [(lambda: None, lambda: mod_finish(1))] + mod_gate_steps(1, 5)
    mod_calls = []
    for i_ in range(len(msteps)):
        def call(i_=i_):
            if i_ + 1 < len(msteps):
                msteps[i_ + 1][0]()
            msteps[i_][1]()
        mod_calls.append(call)
    msteps[0][0]()
    for _ in range(5):
        mod_calls.pop(0)()
    RM = [b_mod_]
    late_mod = mod_calls

    def rstd_ops(src, dst, inv_n, b_stat):
        add("act", lambda e: e.activation(out=dst, in_=src, func=AF.Ln, scale=inv_n, bias=epsT[:, 0:1]),
            reads=[b_stat, b_eps], writes=[b_stat])
        add("act", lambda e: e.activation(out=dst, in_=dst, func=AF.Exp, scale=-0.5), reads=[b_stat], writes=[b_stat])

    class Pipe:
        pass

    def make_pipe(nxs, with_qk, nht=2, nxn=2):
        P = Pipe()
        P.XS = [A.alloc([128, 1024], F32) for _ in range(nxs)]
        P.b_xs = S.bufs("xs", nxs)
        P.d_xs = [S.dsem(uname("xs")) for i in range(nxs)]
        P.XN = [A.alloc([128, 1024], BF16) for _ in range(nxn)]
        P.b_xn = S.bufs("xn", max(nxn, 1))
        P.HT = [A.alloc([128, 8, 512], BF16) for _ in range(nht)]
        P.b_ht = [[[S.buf("ht", rw=False) for _ in range(2)] for _ in range(4)] for _ in range(nht)]
        P.st = [A.alloc([128, 4], F32) for _ in range(4)]
        P.b_st = S.bufs("st", 4)
        P.ctr = 0
        P.gctr = 0
        P.actr = 0
        return P

    def norm_transpose(P, src_tile, b_src, t, hslot, n, v):
        c = P.ctr
        P.ctr += 1
        st = P.st[c % 4]
        bst = P.b_st[c % 4]
        xn = P.XN[c % 2]
        bxn = P.b_xn[c % 2]
        add("act", lambda e: e.activation(out=junk, in_=src_tile, func=AF.Square, accum_out=st[:, 0:1]),
            reads=[b_src], writes=[bst])
        rstd_ops(st[:, 0:1], st[:, 1:2], 1.0 / 1024, bst)
        add("dve", lambda e: e.tensor_scalar(out=xn, in0=src_tile, scalar1=st[:, 1:2], scalar2=None, op0=ALU.mult),
            reads=[b_src, bst], writes=[bxn])
        pf, pb, bb, _bi = nb()
        for k in range(8):
            add("pe", lambda e, k=k: e.transpose(out=pb[:, k * 128:(k + 1) * 128], in_=xn[:, k * 128:(k + 1) * 128],
                                                 identity=ident), reads=[bxn] + RC, writes=[bb])
        ht = P.HT[hslot]
        for k in range(8):
            par = k % 2
            bh = P.b_ht[hslot][t][par]
            if par == 0:
                add("act", lambda e, k=k: e.activation(
                    out=ht[:, k, t * 128:(t + 1) * 128], in_=pb[:, k * 128:(k + 1) * 128], func=AF.Identity,
                    scale=gmod[:, n, k, v:v + 1], bias=modcol[:, 2 * n, k, v:v + 1]), reads=RM, writes=[bb, bh])
            else:
                add("dve", lambda e, k=k: e.tensor_scalar(
                    out=ht[:, k, t * 128:(t + 1) * 128], in0=pb[:, k * 128:(k + 1) * 128],
                    scalar1=gmod[:, n, k, v:v + 1], scalar2=modcol[:, 2 * n, k, v:v + 1], op0=ALU.mult, op1=ALU.add),
                    reads=RM, writes=[bb, bh])

    def attention_multi(T, jobs):
        steps = []
        for ji, J in enumerate(jobs):
            nkt = len(J["kts"])
            G = max(1, 512 // J["nq"])
            for kt0 in range(0, nkt, G):
                ktl = list(range(kt0, min(nkt, kt0 + G)))
                for m in range(2):
                    steps.append((ji, ktl, m, ktl[-1] == nkt - 1 and m == 1))
        jst = {}

        def job_state(ji):
            if ji in jst:
                return jst[ji]
            J = jobs[ji]
            nq = J["nq"]
            st = {}
            qs = T.qctr % 2
            T.qctr += 1
            st["qp"] = T.QP[qs]
            st["bqp"] = T.b_qp[qs]
            for m in range(2):
                add("pool", lambda e, m=m, qp=st["qp"], q=J["q"], nq=nq: e.tensor_copy(
                    out=qp[m][m * 64:(m + 1) * 64, 0:nq], in_=q[m * 64:(m + 1) * 64, :]),
                    reads=[T.b_q], writes=[st["bqp"][m]])
            jst[ji] = st
            return st

        def alloc_acc(ji):
            st = jst[ji]
            nqt = jobs[ji]["nq"] // 128
            nbk = (nqt * 2 + 2) // 3
            st["accb"] = [nb() for _ in range(nbk)]
            for b_ in st["accb"]:
                reserved.add(b_[3])
            st["started"] = [False] * nbk

        stb = {}

        def emit_st(i):
            ji, ktl, m, _l = steps[i]
            st = job_state(ji)
            J = jobs[ji]
            nq = J["nq"]
            pf, pb, bb, _bi = nb()
            for l, kt in enumerate(ktl):
                kT, bk, vx, bv = J["kts"][kt]
                add("pe", lambda e, kT=kT, pf=pf, qp=st["qp"][m], nq=nq, l=l: e.matmul(
                    pf[:, l * nq:(l + 1) * nq], lhsT=kT, rhs=qp[:, 0:nq], start=True, stop=True),
                    reads=[bk, st["bqp"][m]], writes=[bb])
            stb[i] = (pf, bb, _bi)
            reserved.add(_bi)

        DEPTH = int(os.environ.get('MK_DEPTH', '3'))
        for i0 in range(min(DEPTH, len(steps))):
            emit_st(i0)
        for i, (ji, ktl, m, lastj) in enumerate(steps):
            if i + DEPTH < len(steps):
                emit_st(i + DEPTH)
            J = jobs[ji]
            nq = J["nq"]
            nqt = nq // 128
            st = jst[ji]
            if "accb" not in st:
                alloc_acc(ji)
                if ji + 1 < len(jobs):
                    job_state(ji + 1)
            pf, bb, _sbi = stb.pop(i)
            ps = T.pctr % len(T.PT)
            T.pctr += 1
            pt = T.PT[ps]
            bpt = T.b_pt[ps]
            wcols = len(ktl) * nq
            add("act", lambda e, pf=pf, pt=pt, wcols=wcols: e.activation(out=pt[:, 0:wcols], in_=pf[:, 0:wcols], func=AF.Exp, scale=0.125),
                reads=[], writes=[bb, bpt])
            reserved.discard(_sbi)
            for qt in range(nqt):
                a = qt * 2 + m
                bi, ci = a // 3, (a % 3) * 129
                af, _, ab, _x = st["accb"][bi]
                for l, kt in enumerate(ktl):
                    kT, bk, vx, bv = J["kts"][kt]
                    lastk = (kt == len(J["kts"]) - 1)
                    st_flag = not st["started"][bi]
                    st["started"][bi] = True
                    add("pe", lambda e, af=af, ci=ci, pt=pt, qt=qt, vx=vx, st_flag=st_flag, lastk=lastk, l=l, nq=nq: e.matmul(
                        af[:, ci:ci + 129], lhsT=pt[:, l * nq + qt * 128: l * nq + (qt + 1) * 128], rhs=vx,
                        start=st_flag, stop=lastk, skip_group_check=True), reads=[bpt, bv], writes=[ab])
            if lastj:
                pend = att_post(T, J, st)
                att_flush_evacs(T)
                if T.pending is not None:
                    att_post2(T, T.pending)
                T.pending = pend
                yield
        if T.pending is not None:
            att_post2(T, T.pending)
            T.pending = None
        att_flush_evacs(T)
        yield

    def att_post(T, J, st):
        nq, h = J["nq"], J["h"]
        nqt = nq // 128
        accb = st["accb"]
        sc = T.sctr % len(T.ast)
        T.sctr += 1
        ss = T.ast[sc]
        bs = T.b_ast[sc]
        aset = T.acctr % 2
        T.acctr += 1
        accs = []
        for bi, (af, _, ab, _x) in enumerate(accb):
            na = min(3, nqt * 2 - bi * 3)
            sb = T.ACCS[aset][bi]
            bsb = T.b_accs[aset][bi]
            if T.acc_eng == "act":
                add("act", lambda e, af=af, sb=sb, na=na: e.copy(out=sb[:, 0:129 * na], in_=af[:, 0:129 * na]),
                    reads=[], writes=[ab, bsb])
            else:
                add("dve", lambda e, af=af, sb=sb, na=na: e.tensor_copy(out=sb[:, 0:129 * na], in_=af[:, 0:129 * na]),
                    reads=[], writes=[ab, bsb])
            accs.append((sb, bsb))
        for b_ in accb:
            reserved.discard(b_[3])
        for bi, (sb, bsb) in enumerate(accs):
            na = min(3, nqt * 2 - bi * 3)
            add("dve", lambda e, sb=sb, bi=bi, na=na, ss=ss: e.reciprocal(
                out=ss[:, bi * 3: bi * 3 + na], in_=sb[:, 128:128 + 129 * (na - 1) + 1:129]), reads=[bsb], writes=[bs])
        add("dve", lambda e, ss=ss: e.tensor_tensor(
            out=ss[:, 8:8 + nqt], in0=ss[:, 1:2 * nqt:2], in1=NLAM.broadcast_to([128, nqt]), op=ALU.mult),
            reads=[b_lam], writes=[bs])
        oos = []
        for qt in range(nqt):
            a0, a1 = qt * 2, qt * 2 + 1
            sb0, bsb0 = accs[a0 // 3]
            sb1, bsb1 = accs[a1 // 3]
            c0, c1 = (a0 % 3) * 129, (a1 % 3) * 129
            oc = T.octr % len(T.OO)
            T.octr += 1
            O0 = T.O0[oc % len(T.O0)]
            bo0 = T.b_o0[oc % len(T.O0)]
            OO, boo = T.OO[oc], T.b_oo[oc]
            add("dve", lambda e, sb0=sb0, c0=c0, ss=ss, O0=O0, a0=a0: e.tensor_scalar(
                out=O0, in0=sb0[:, c0:c0 + 128], scalar1=ss[:, a0:a0 + 1], scalar2=None, op0=ALU.mult),
                reads=[bs, bsb0], writes=[bo0])
            add("dve", lambda e, sb1=sb1, c1=c1, ss=ss, O0=O0, OO=OO, qt=qt: e.scalar_tensor_tensor(
                out=OO, in0=sb1[:, c1:c1 + 128], scalar=ss[:, 8 + qt:9 + qt], in1=O0, op0=ALU.mult, op1=ALU.add),
                reads=[bs, bo0, bsb1], writes=[boo])
            add("pool", lambda e, OO=OO, O0=O0: e.tensor_tensor(out=O0, in0=OO, in1=OO, op=ALU.mult),
                reads=[boo], writes=[bo0])
            add("dve", lambda e, O0=O0, ss=ss, qt=qt: e.tensor_reduce(out=ss[:, 12 + qt:13 + qt], in_=O0, axis=AX.X, op=ALU.add),
                reads=[bo0], writes=[bs])
            oos.append((OO, boo))
        return dict(J=J, ss=ss, bs=bs, oos=oos)

    def att_post2(T, pend):
        J, ss, bs, oos = pend["J"], pend["ss"], pend["bs"], pend["oos"]
        nq, h, cat_col0 = J["nq"], J["h"], J["col0"]
        nqt = nq // 128
        add("act", lambda e: e.activation(out=ss[:, 16:16 + nqt], in_=ss[:, 12:12 + nqt], func=AF.Ln, scale=1.0 / 128,
                                          bias=epsT[:, 0:1]), reads=[bs, b_eps], writes=[bs])
        add("act", lambda e: e.activation(out=ss[:, 16:16 + nqt], in_=ss[:, 16:16 + nqt], func=AF.Exp, scale=-0.5),
            reads=[bs], writes=[bs])
        for qt in range(nqt):
            OO, boo = oos[qt]
            at = T.AT[(T.actr + qt) % len(T.AT)]
            bat = T.b_at[(T.actr + qt) % len(T.AT)]
            add(T.fin_eng, lambda e, OO=OO, at=at, qt=qt: e.scalar_tensor_tensor(
                out=at[:, h * 128:(h + 1) * 128], in0=OO, scalar=ss[:, 16 + qt:17 + qt], in1=subgB, op0=ALU.mult, op1=ALU.mult),
                reads=[boo, bs, b_subg], writes=[bat])
        if h == 3:
            for qt in range(nqt):
                at = T.AT[(T.actr + qt) % len(T.AT)]
                bat = T.b_at[(T.actr + qt) % len(T.AT)]
                pf, pb, bb, _bi = nb()
                for hh in range(4):
                    add("pe", lambda e, hh=hh, at=at, pb=pb: e.transpose(
                        out=pb[:, hh * 128:(hh + 1) * 128], in_=at[:, hh * 128:(hh + 1) * 128], identity=ident),
                        reads=[bat] + RC, writes=[bb])
                col = cat_col0 + qt * 128
                reserved.add(_bi)
                T.evacs.append((pb, bb, _bi, col))
            T.actr += nqt

    def att_flush_evacs(T):
        while T.evacs:
            pb, bb, _bi, col = T.evacs.pop(0)
            add("dve", lambda e, pb=pb, col=col: e.tensor_copy(
                out=CAT[:, 0:4, col:col + 128], in_=pb[:, 0:512].rearrange("p (a b) -> p a b", b=128)),
                reads=[], writes=[bb, T.b_cat])
            reserved.discard(_bi)

    class Att:
        pass

    def make_att(npt, nat, noo, nq=512):
        T = Att()
        T.PT = [A.alloc([128, 512], BF16) for _ in range(npt)]
        T.b_pt = S.bufs("pt", npt)
        T.pctr = 0
        nbk = (nq // 128 * 2 + 2) // 3
        T.ACCS = [[A.alloc([128, 388], F32) for _ in range(nbk)] for _ in range(2)]
        T.b_accs = [S.bufs("accs", nbk) for _ in range(2)]
        T.acctr = 0
        T.ast = [A.alloc([128, 32], F32) for _ in range(4)]
        T.b_ast = S.bufs("ast", 4)
        T.sctr = 0
        T.O0 = [A.alloc([128, 128], F32) for _ in range(2)]
        T.OO = [A.alloc([128, 128], F32) for _ in range(noo)]
        T.b_o0 = S.bufs("o0", 2)
        T.b_oo = S.bufs("oo", noo)
        T.octr = 0
        T.pending = None
        T.evacs = []
        T.acc_eng = 'dve'
        T.fin_eng = 'dve'
        T.AT = [A.alloc([128, 512], BF16) for _ in range(nat)]
        T.b_at = S.bufs("at", nat, rw=False)
        T.actr = 0
        T.b_cat = S.buf("cat", rw=False)
        T.b_q = S.buf("qT")
        T.QP = [[A.alloc([128, nq], BF16) for _ in range(2)] for _ in range(2)]
        T.b_qp = [S.bufs("qp", 2) for _ in range(2)]
        T.qctr = 0
        for qs in range(2):
            for m in range(2):
                add("pool", lambda e, qs=qs, m=m: e.memset(T.QP[qs][m], 0.0), reads=[], writes=[T.b_qp[qs][m]])
        return T

    class QK:
        pass

    def make_qk(with_rope=True, nvo=2, nko=2):
        Q = QK()
        Q.SQ = [A.alloc([128, 512], F32) for _ in range(2)]
        Q.b_sq = S.bufs("sq", 2)
        Q.QN = [A.alloc([128, 512], F32) for _ in range(2)]
        Q.b_qn = S.bufs("qn", 2)
        Q.QB = [A.alloc([128, 512], BF16) for _ in range(6)]
        Q.b_qb = S.bufs("qb", 6)
        Q.bctr = 0
        Q.RT = [A.alloc([128, 256], F32) for _ in range(4)] if with_rope else []
        Q.b_rt = S.bufs("rt", 4)
        Q.st = [A.alloc([128, 16], F32) for _ in range(2)]
        Q.b_st = S.bufs("qst", 2)
        Q.KO = [A.alloc([128, 512], F32) for _ in range(nko)]
        Q.b_ko = S.bufs("ko", max(nko, 1))
        Q.d_ko = [S.dsem(uname("ko")) for i in range(nko)]
        Q.VO = [A.alloc([128, 512], F32) for _ in range(nvo)]
        Q.b_vo = S.bufs("vo", max(nvo, 1))
        Q.d_vo = [S.dsem(uname("vo")) for i in range(nvo)]
        Q.ctr = 0
        Q.kctr = 0
        Q.vctr = 0
        return Q

    def qk_D(Q, bank, gBc, rope_tile, kout=None):
        pf, _, bb, _x = bank
        c = Q.ctr % 2
        Q.ctr += 1
        cb = Q.bctr % 6
        Q.bctr += 1
        sq, bsq = Q.SQ[c], Q.b_sq[c]
        qn, bqn = Q.QN[c], Q.b_qn[c]
        qb, bqb = Q.QB[cb], Q.b_qb[cb]
        st, bst = Q.st[c], Q.b_st[c]
        add("act", lambda e: e.activation(out=sq, in_=pf, func=AF.Square), reads=[], writes=[bb, bsq])
        add("dve", lambda e: e.tensor_reduce(out=st[:, 0:8], in_=sq.rearrange("p (a d) -> p a d", d=64), axis=AX.X, op=ALU.add),
            reads=[bsq], writes=[bst])
        rstd_ops(st[:, 0:8], st[:, 8:16], 1.0 / 64, bst)
        add("dve", lambda e: e.tensor_tensor(
            out=qn.rearrange("p (a d) -> p a d", d=64), in0=pf.rearrange("p (a d) -> p a d", d=64),
            in1=st[:, 8:16].unsqueeze(2).broadcast_to([128, 8, 64]), op=ALU.mult), reads=[bst], writes=[bb, bqn])
        gbc = gBc.unsqueeze(1).broadcast_to([128, 8, 64])
        if rope_tile is None:
            if kout is not None:
                kc = Q.kctr % len(Q.KO)
                Q.kctr += 1
                ko, bko, dko = Q.KO[kc], Q.b_ko[kc], Q.d_ko[kc]
                add("pool", lambda e: e.tensor_tensor(out=ko.rearrange("p (a d) -> p a d", d=64),
                                                      in0=qn.rearrange("p (a d) -> p a d", d=64), in1=gbc, op=ALU.mult),
                    reads=[bqn] + RC, writes=[bko])
                S.add("sp", lambda e: e.dma_start(out=kout, in_=ko), reads=[bko], dsem=dko)
                add("act", lambda e: e.copy(out=qb, in_=ko), reads=[bko], writes=[bqb])
            else:
                add("dve", lambda e: e.tensor_copy(out=qb, in_=qn), reads=[bqn], writes=[bqb])
        else:
            add("pool", lambda e: e.tensor_tensor(out=qn.rearrange("p (a d) -> p a d", d=64),
                                                  in0=qn.rearrange("p (a d) -> p a d", d=64), in1=gbc, op=ALU.mult),
                reads=RC, writes=[bqn])
            xv = qn.rearrange("p (h x j f) -> p h x j f", h=8, x=2, j=2, f=16)
            ov = qb.rearrange("p (h x j f) -> p h x j f", h=8, x=2, j=2, f=16)
            X1, X2 = xv[:, :, :, 0, :], xv[:, :, :, 1, :]
            O1, O2 = ov[:, :, :, 0, :], ov[:, :, :, 1, :]
            cosB = rope[:, rope_tile, 0, :].rearrange("p (x f) -> p x f", f=16).unsqueeze(1).broadcast_to([128, 8, 2, 16])
            sinB = rope[:, rope_tile, 1, :].rearrange("p (x f) -> p x f", f=16).unsqueeze(1).broadcast_to([128, 8, 2, 16])
            T1, T2, T3, T4 = [r.rearrange("p (h x f) -> p h x f", h=8, x=2, f=16) for r in Q.RT]
            b1, b2, b3, b4 = Q.b_rt
            add("dve", lambda e: e.tensor_tensor(out=T1, in0=X1, in1=cosB, op=ALU.mult), reads=[bqn, b_rope] + RC, writes=[b1])
            add("pool", lambda e: e.tensor_tensor(out=T2, in0=X2, in1=sinB, op=ALU.mult), reads=[bqn, b_rope] + RC, writes=[b2])
            add("pool", lambda e: e.tensor_tensor(out=T3, in0=X2, in1=cosB, op=ALU.mult), reads=[bqn, b_rope] + RC, writes=[b3])
            add("dve", lambda e: e.tensor_tensor(out=T4, in0=X1, in1=sinB, op=ALU.mult), reads=[bqn, b_rope] + RC, writes=[b4])
            add("dve", lambda e: e.tensor_tensor(out=O1, in0=T1, in1=T2, op=ALU.subtract), reads=[b1, b2], writes=[bqb])
            add("pool", lambda e: e.tensor_tensor(out=O2, in0=T3, in1=T4, op=ALU.add), reads=[b3, b4], writes=[bqb])
        return qb, bqb

    def qk_E(items):
        tf, tb, tbb, _bi = nb()
        for ii, (qb, bqb, dstT, b_dst, _sc) in enumerate(items):
            for hh in range(4):
                add("pe", lambda e, hh=hh, ii=ii, qb=qb: e.transpose(
                    out=tb[:, ii * 512 + hh * 128: ii * 512 + (hh + 1) * 128], in_=qb[:, hh * 128:(hh + 1) * 128],
                    identity=ident), reads=[bqb] + RC, writes=[tbb])
        for ii, (qb, bqb, dstT, b_dst, _sc) in enumerate(items):
            src = tb[:, ii * 512:(ii + 1) * 512].rearrange("p (a b) -> p a b", b=128)
            if ii == 0 and len(items) == 2 and items[0][4]:
                add("act", lambda e, src=src, dstT=dstT: e.activation(out=dstT, in_=src, func=AF.Identity, scale=gqcol[:, 0:1]),
                    reads=RC, writes=[tbb, b_dst])
            elif ii == 0:
                add("act", lambda e, src=src, dstT=dstT: e.copy(out=dstT, in_=src), reads=[], writes=[tbb, b_dst])
            else:
                add("dve", lambda e, src=src, dstT=dstT: e.tensor_copy(out=dstT, in_=src), reads=[], writes=[tbb, b_dst])

    def front_pipeline(P, Q, W, jobs, FT_of, b_FT, b_QT, b_KT, group_end=None, extra_steps=None, bg_rate=3, after_loop=None):
        n = len(jobs)
        stt = [dict() for _ in range(n)]

        def stage_A(i):
            J, st = jobs[i], stt[i]
            xsl = P.actr % len(P.XS)
            P.actr += 1
            xt, bx, dx = P.XS[xsl], P.b_xs[xsl], P.d_xs[xsl]
            S.add("sp", lambda e, xt=xt, J=J: e.dma_start(out=xt, in_=J["x"]), writes=[bx], dsem=dx)
            c = P.ctr
            P.ctr += 1
            sst = P.st[c % 4]
            bst = P.b_st[c % 4]
            xn = P.XN[c % 2]
            bxn = P.b_xn[c % 2]
            add("act", lambda e: e.activation(out=junk, in_=xt, func=AF.Square, accum_out=sst[:, 0:1]),
                reads=[bx], writes=[bst])
            rstd_ops(sst[:, 0:1], sst[:, 1:2], 1.0 / 1024, bst)
            add("dve", lambda e: e.tensor_scalar(out=xn, in0=xt, scalar1=sst[:, 1:2], scalar2=None, op0=ALU.mult),
                reads=[bx, bst], writes=[bxn])
            st["xn"], st["bxn"] = xn, bxn

        def stage_B(i):
            J, st = jobs[i], stt[i]
            hslot = J["grp"] % 2
            t = J["t"]
            v = J["v"]
            xn, bxn = st["xn"], st["bxn"]
            pf, pb, bb, _bi = nb()
            for k in range(8):
                add("pe", lambda e, k=k: e.transpose(out=pb[:, k * 128:(k + 1) * 128], in_=xn[:, k * 128:(k + 1) * 128],
                                                     identity=ident), reads=[bxn] + RC, writes=[bb])
            ht = P.HT[hslot]
            bh0, bh1 = P.b_ht[hslot][t]
            htv = ht[:, :, t * 128:(t + 1) * 128]
            add("dve", lambda e: e.tensor_tensor(
                out=htv, in0=pb[:, 0:1024].rearrange("p (k c) -> p k c", c=128),
                in1=gmod[:, 0, :, v:v + 1].broadcast_to([128, 8, 128]), op=ALU.mult), reads=RM, writes=[bb, bh0])
            add("pool", lambda e: e.tensor_tensor(
                out=htv, in0=htv, in1=modcol[:, 0, :, v:v + 1].broadcast_to([128, 8, 128]), op=ALU.add),
                reads=RM + [bh0], writes=[bh0, bh1])

        def stage_C(i):
            J, st = jobs[i], stt[i]
            hslot = J["grp"] % 2
            t = J["t"]
            ht = P.HT[hslot]
            bh = P.b_ht[hslot][t]
            cgs = ([0] if J["need_q"] else []) + [1, 2]
            bk_ = {cg: nb() for cg in cgs}
            for k in range(8):
                for cg in cgs:
                    add("pe", lambda e, k=k, cg=cg, bkf=bk_[cg][0]: e.matmul(
                        bkf, lhsT=ht[:, k, t * 128:(t + 1) * 128], rhs=W.WIN[:, k, cg * 512:(cg + 1) * 512],
                        start=(k == 0), stop=(k == 7)), reads=[bh[0], bh[1], W.b_win[k]], writes=[bk_[cg][2]])
            st["banks"] = bk_
            for b_ in bk_.values():
                reserved.add(b_[3])
            if t == 3:
                FT, fcol0 = FT_of(J["grp"])
                allh = [b for tt in range(4) for b in P.b_ht[hslot][tt]]
                for g in range(4):
                    pf, pb, bb, _bi = nb()
                    for k in range(8):
                        add("pe", lambda e, g=g, k=k, pf=pf: e.matmul(
                            pf, lhsT=W.WIN[:, k, 1536 + g * 128: 1536 + (g + 1) * 128], rhs=ht[:, k, :],
                            start=(k == 0), stop=(k == 7)), reads=allh + [W.b_win[k]], writes=[bb])
                    if g % 2 == 0:
                        add("act", lambda e, g=g, pf=pf: e.copy(out=FT[:, g, fcol0:fcol0 + 512], in_=pf), reads=[], writes=[bb, b_FT])
                    else:
                        add("dve", lambda e, g=g, pf=pf: e.tensor_copy(out=FT[:, g, fcol0:fcol0 + 512], in_=pf), reads=[], writes=[bb, b_FT])

        def stage_D(i):
            J, st = jobs[i], stt[i]
            bk_ = st["banks"]
            items = []
            if J["need_q"]:
                qb, bqb = qk_D(Q, bk_[0], gqB, J["rope"])
                items.append((qb, bqb, J["qdst"], b_QT, J["rope"] is None))
            qb, bqb = qk_D(Q, bk_[1], gkB, J["rope"], kout=J["kout"])
            items.append((qb, bqb, J["kdst"], b_KT, False))
            st["items"] = items
            vf, _, vbb, _x = bk_[2]
            vx = J["vx"]
            add("act", lambda e, vf=vf, vx=vx: e.copy(out=vx[:, :, 0:128], in_=vf.rearrange("p (a b) -> p a b", b=128)),
                reads=[], writes=[vbb, J["b_vx"]])
            if J["vout"] is not None:
                vc = Q.vctr % len(Q.VO)
                Q.vctr += 1
                vo, bvo, dvo = Q.VO[vc], Q.b_vo[vc], Q.d_vo[vc]
                add("dve", lambda e, vf=vf, vo=vo: e.tensor_copy(out=vo, in_=vf), reads=[], writes=[vbb, bvo])
                S.add("sp", lambda e, vo=vo, J=J: e.dma_start(out=J["vout"], in_=vo), reads=[bvo], dsem=dvo)
            for b_ in bk_.values():
                reserved.discard(b_[3])

        bg = []

        def run_bg(nunits):
            while nunits > 0 and bg:
                try:
                    next(bg[0])
                    nunits -= 1
                except StopIteration:
                    bg.pop(0)

        for s_ in range(-3, n + 2):
            run_bg(bg_rate)
            if 0 <= s_ + 3 < n:
                stage_A(s_ + 3)
            if 0 <= s_ + 2 < n:
                stage_B(s_ + 2)
            if 0 <= s_ < n:
                stage_C(s_)
                stage_D(s_)
            if 0 <= s_ - 2 < n:
                qk_E(stt[s_ - 2]["items"])
            if 0 <= s_ < n and extra_steps:
                extra_steps.pop(0)()
            if 0 <= s_ - 2 < n and group_end is not None:
                for pri, gen in group_end(jobs[s_ - 2]):
                    if pri:
                        bg.insert(0, gen)
                    else:
                        bg.append(gen)
        if after_loop is not None:
            while extra_steps:
                extra_steps.pop(0)()
            after_loop()
        run_bg(1 << 30)

    def uv_tiles(FT, b_FT, fcol0, ntiles, UV, b_UV, uvt0):
        for t in range(ntiles):
            for gp in range(2):
                pf, pb, bb, _bi = nb()
                for gi in range(2):
                    g = gp * 2 + gi
                    add("pe", lambda e, g=g, gi=gi, t=t, pf=pf: e.matmul(
                        pf[:, gi * 256:(gi + 1) * 256], lhsT=FT[:, g, fcol0 + t * 128: fcol0 + (t + 1) * 128],
                        rhs=wcs[:, g, :], start=True, stop=True), reads=[b_FT, b_wcs], writes=[bb])
                eng = "dve" if (t + gp) % 2 == 0 else "act"
                dst = UV[:, uvt0 + t, gp * 2:gp * 2 + 2, :]
                if eng == "dve":
                    add("dve", lambda e, pf=pf, dst=dst: e.tensor_copy(out=dst, in_=pf.rearrange("p (a b) -> p a b", b=256)),
                        reads=[], writes=[bb, b_UV])
                else:
                    add("act", lambda e, pf=pf, dst=dst: e.copy(out=dst, in_=pf.rearrange("p (a b) -> p a b", b=256)),
                        reads=[], writes=[bb, b_UV])

    class Wt:
        pass

    def load_win(W):
        for k in range(8):
            S.add("pool", lambda e, k=k: e.dma_start(out=W.WIN[:, k, :], in_=w_in[k * 128:(k + 1) * 128, :]),
                  writes=[W.b_win[k]], dsem=W.d_win[k])

    PRE_S = {}

    def back_phase(xsrc_all, v, ydst, tagn, prefetch=False, win_prefetch=False, pf_w1=True):
        A.off = mod_mark
        mark = A.off
        XM0 = A.alloc([128, 4, 1024], F32)
        WOUT = A.alloc([128, 8, 1024], BF16)
        b_wo = S.buf("wout")
        d_wo = S.dsem(f"wo{tagn}")
        P = make_pipe(0, False, 1, 0)
        W2 = [A.alloc([128, 4, 1024], BF16) for _ in range(3)]
        b_w2 = S.bufs("w2s", 3)
        d_w2 = [S.dsem(f"w2{i}{tagn}") for i in range(3)]
        assert A.off <= WIN_S_OFF, (A.off, WIN_S_OFF)
        A.off = WIN_S_OFF
        W1 = [A.alloc([128, 8, 512], BF16) for _ in range(3)]
        b_w1 = S.bufs("w1s", 3)
        d_w1 = [S.dsem(f"w1{i}{tagn}") for i in range(3)]
        XNB = [A.alloc([128, 1024], BF16) for _ in range(4)]
        b_xnb = S.bufs("xnb", 4)
        assert A.off == WIN_S_OFF + 32768, A.off
        HID = A.alloc([128, 32, 512], BF16)
        b_hid = S.bufs("hid", 32, rw=False)
        XM1 = A.alloc([128, 4, 1024], F32)
        XMID2 = [XM0, XM1]
        b_xm2 = [S.bufs("xm", 4) for _ in range(2)]
        d_xm2 = [[S.dsem(f"xm{i}{g_}{tagn}") for i in range(4)] for g_ in range(2)]
        TM = [A.alloc([128, 512], F32) for _ in range(2)]
        b_tm = S.bufs("tm", 2)
        tmc = [0]
        RL = [A.alloc([128, 512], F32) for _ in range(3)]
        b_rl = S.bufs("rl", 3)
        rlc = [0]
        b_cat = S.buf("catr")
        w1c = [0]
        w2c = [0]
        snap_ = [None]

        def load_w1(jb):
            s = w1c[0] % 3
            w1c[0] += 1
            S.add("pool", lambda e: e.dma_start(out=W1[s], in_=w1[:, jb * 512:(jb + 1) * 512].rearrange("(k p) n -> p k n", p=128)),
                  writes=[b_w1[s]], dsem=d_w1[s], snap=snap_[0])
            return s

        def load_w2(jb):
            s = w2c[0] % 3
            w2c[0] += 1
            S.add("pool", lambda e: e.dma_start(out=W2[s], in_=w2[jb * 512:(jb + 1) * 512, :].rearrange("(j p) n -> p j n", p=128)),
                  writes=[b_w2[s]], dsem=d_w2[s])
            return s

        def evac_gate(pf, bb, gi, half, t, XMID, b_xm):
            i = tmc[0] % 2
            tmc[0] += 1
            tm, btm = TM[i], b_tm[i]
            add("dve", lambda e: e.tensor_tensor(out=tm, in0=pf, in1=gB[:, v, gi, half * 512:(half + 1) * 512], op=ALU.mult),
                reads=[b_gB], writes=[bb, btm])
            add("pool", lambda e: e.tensor_tensor(out=XMID[:, t, half * 512:(half + 1) * 512],
                                                  in0=XMID[:, t, half * 512:(half + 1) * 512], in1=tm, op=ALU.add),
                reads=[btm], writes=[b_xm[t]])

        ht = P.HT[0]
        allh = [b for t in range(4) for b in P.b_ht[0][t]]
        w1q = []
        w2q = []

        def xload(grp):
            r0 = grp * 512
            XMID, b_xm, d_xm = XMID2[grp], b_xm2[grp], d_xm2[grp]
            for t in range(4):
                S.add("sp", lambda e, t=t, r0=r0, XMID=XMID: e.dma_start(out=XMID[:, t, :], in_=xsrc_all[r0 + t * 128: r0 + (t + 1) * 128, :]),
                      writes=[b_xm[t]], dsem=d_xm[t], snap=snap_[0])

        def wout(grp):
            r0 = grp * 512
            XMID, b_xm = XMID2[grp], b_xm2[grp]
            S.mark(f'b{tagn}_wout{grp}')
            for t in range(4):
                b0, b1 = nb(), nb()
                for k in range(8):
                    for half, bk in ((0, b0), (1, b1)):
                        add("pe", lambda e, k=k, half=half, bk=bk, t=t, r0=r0: e.matmul(
                            bk[0], lhsT=CAT[:, k, r0 + t * 128: r0 + (t + 1) * 128], rhs=WOUT[:, k, half * 512:(half + 1) * 512],
                            start=(k == 0), stop=(k == 7)), reads=[b_cat, b_wo], writes=[bk[2]])
                evac_gate(b0[0], b0[2], 0, 0, t, XMID, b_xm)
                evac_gate(b1[0], b1[2], 0, 1, t, XMID, b_xm)

        def norm_part(grp):
            XMID, b_xm = XMID2[grp], b_xm2[grp]
            for t in range(4):
                c = P.ctr
                P.ctr += 1
                st = P.st[c % 4]
                bst = P.b_st[c % 4]
                xt = XMID[:, t, :]
                add("act", lambda e, xt=xt, st=st: e.activation(out=junk, in_=xt, func=AF.Square, accum_out=st[:, 0:1]),
                    reads=[b_xm[t]], writes=[bst])
                rstd_ops(st[:, 0:1], st[:, 1:2], 1.0 / 1024, bst)
                add("dve", lambda e, xt=xt, st=st, t=t: e.tensor_scalar(out=XNB[t], in0=xt, scalar1=st[:, 1:2], scalar2=None, op0=ALU.mult),
                    reads=[b_xm[t], bst], writes=[b_xnb[t]])

        def trans_part(grp):
            for t in range(4):
                pf, pb, bb, _bi = nb()
                for k in range(8):
                    add("pe", lambda e, k=k, t=t, pb=pb: e.transpose(out=pb[:, k * 128:(k + 1) * 128], in_=XNB[t][:, k * 128:(k + 1) * 128],
                                                                     identity=ident), reads=[b_xnb[t]] + RC, writes=[bb])
                for k in range(8):
                    par = k % 2
                    bh = P.b_ht[0][t][par]
                    if t % 2 == 0:
                        add("act", lambda e, k=k, t=t, pb=pb: e.activation(
                            out=ht[:, k, t * 128:(t + 1) * 128], in_=pb[:, k * 128:(k + 1) * 128], func=AF.Identity,
                            scale=gmod[:, 1, k, v:v + 1], bias=modcol[:, 2, k, v:v + 1]), reads=RM, writes=[bb, bh])
                    else:
                        add("dve", lambda e, k=k, t=t, pb=pb: e.tensor_scalar(
                            out=ht[:, k, t * 128:(t + 1) * 128], in0=pb[:, k * 128:(k + 1) * 128],
                            scalar1=gmod[:, 1, k, v:v + 1], scalar2=modcol[:, 2, k, v:v + 1], op0=ALU.mult, op1=ALU.add),
                            reads=RM, writes=[bb, bh])

        def w1_stage(grp, mid=None, mid2=None):
            S.mark(f'b{tagn}_w1_{grp}')
            for jb in range(8):
                if jb == 4 and mid is not None:
                    mid()
                if jb == 7 and mid2 is not None:
                    mid2()
                s_ = w1q.pop(0)
                if jb + 2 < 8:
                    w1q.append(load_w1(jb + 2))
                if jb == 3:
                    w2q.append(load_w2(0))
                if jb == 6:
                    w2q.append(load_w2(1))
                for jj in range(4):
                    pf, pb, bb, _bi = nb()
                    for k in range(8):
                        add("pe", lambda e, s_=s_, jj=jj, k=k, pf=pf: e.matmul(
                            pf, lhsT=W1[s_][:, k, jj * 128:(jj + 1) * 128], rhs=ht[:, k, :], start=(k == 0), stop=(k == 7)),
                            reads=allh + [b_w1[s_]], writes=[bb])
                    i = rlc[0] % 3
                    rlc[0] += 1
                    rl, brl = RL[i], b_rl[i]
                    add("act", lambda e, pf=pf, rl=rl: e.activation(out=rl, in_=pf, func=AF.Relu), reads=[], writes=[bb, brl])
                    j = jb * 4 + jj
                    eng = "pool" if j % 4 == 3 else "dve"
                    add(eng, lambda e, rl=rl, j=j: e.tensor_tensor(out=HID[:, j, :], in0=rl, in1=rl, op=ALU.mult),
                        reads=[brl], writes=[b_hid[j]])

        def w2_stage(grp, next_w1):
            r0 = grp * 512
            XMID, b_xm, d_xm = XMID2[grp], b_xm2[grp], d_xm2[grp]
            S.mark(f'b{tagn}_w2_{grp}')
            acc = [[nb(), nb()] for _ in range(4)]
            for jb in range(8):
                s_ = w2q.pop(0)
                if jb + 2 < 8:
                    w2q.append(load_w2(jb + 2))
                if next_w1 and jb in (5, 6):
                    w1q.append(load_w1(jb - 5))
                if (not next_w1) and win_prefetch:
                    kk = jb
                    S.add("pool", lambda e, kk=kk: e.dma_start(out=PRE_S["WIN"][:, kk, :], in_=w_in[kk * 128:(kk + 1) * 128, :]),
                          writes=list(b_w1) + list(b_xnb) + [PRE_S["b_win"][kk]], dsem=PRE_S["d_win"][kk])
                for jj in range(4):
                    j = jb * 4 + jj
                    for t in range(4):
                        for half in range(2):
                            bk = acc[t][half]
                            add("pe", lambda e, s_=s_, jj=jj, j=j, t=t, half=half, bk=bk: e.matmul(
                                bk[0], lhsT=HID[:, j, t * 128:(t + 1) * 128], rhs=W2[s_][:, jj, half * 512:(half + 1) * 512],
                                start=(j == 0), stop=(j == 31)), reads=[b_hid[j], b_w2[s_]], writes=[bk[2]])
            for t in range(4):
                for half in range(2):
                    evac_gate(acc[t][half][0], acc[t][half][2], 1, half, t, XMID, b_xm)
                S.add("sp", lambda e, t=t, r0=r0, XMID=XMID: e.dma_start(out=ydst[r0 + t * 128: r0 + (t + 1) * 128, :], in_=XMID[:, t, :]),
                      reads=[b_xm[t]], dsem=d_xm[t])

        if win_prefetch:
            wv = A.t[:, WIN_S_OFF:WIN_S_OFF + 32768].bitcast(BF16).rearrange("p (a b) -> p a b", b=2048)
            PRE_S["WIN"] = wv
            PRE_S["b_win"] = S.bufs("winS", 8)
            PRE_S["d_win"] = [S.dsem(f"winS{k}") for k in range(8)]
        if prefetch:
            snap_[0] = S.snapshot()
        S.add("pool", lambda e: e.dma_start(out=WOUT, in_=w_out.rearrange("(k p) n -> p k n", p=128)), writes=[b_wo], dsem=d_wo,
              snap=snap_[0])
        if pf_w1:
            w1q.append(load_w1(0))
            w1q.append(load_w1(1))
        xload(0)
        if prefetch:
            snap_[0] = None
            yield
        if not pf_w1:
            w1q.append(load_w1(0))
            w1q.append(load_w1(1))
        xload(1)
        wout(0)
        norm_part(0)
        trans_part(0)
        def mid1():
            wout(1)

        w1_stage(0, mid1, lambda: norm_part(1))
        trans_part(1)
        w2_stage(0, True)
        w1_stage(1)
        w2_stage(1, False)
        S.fence()
        A.off = mark
        yield

    def front_prompt(after_loop=None):
        mark = A.off
        W = Wt()
        W.WIN = A.alloc([128, 8, 2048], BF16)
        W.b_win = S.bufs("win", 8)
        W.d_win = [S.dsem(f"winp{k}") for k in range(8)]
        load_win(W)
        P = make_pipe(2, True)
        Q = make_qk(False, 2, 1)
        T = make_att(3, 4, 4, 256)
        T.acc_eng = 'act'
        QT = [A.alloc([128, 4, 512], BF16) for _ in range(2)]
        KT = [A.alloc([128, 4, 512], BF16) for _ in range(2)]
        VX = [A.alloc([128, 4, 4, 130], BF16) for _ in range(2)]
        FT1 = A.alloc([128, 4, 512], BF16)
        FT = [FT1, FT1]
        UV = A.alloc([128, 4, 4, 256], BF16)
        b_KT, b_FT, b_UV = S.buf("KT"), S.buf("FT"), S.buf("UV")
        b_VX = [S.bufs("VX", 4) for _ in range(2)]
        for gs in range(2):
            add("pool", lambda e, gs=gs: e.memset(VX[gs][:, :, :, 128:130], 1.0), reads=[], writes=b_VX[gs])
        jobs = []
        for grp in range(2):
            for t in range(4):
                r = grp * 512 + t * 128
                jobs.append(dict(x=xp[r:r + 128, :], v=0, need_q=True, rope=None,
                                 qdst=QT[grp][:, :, t * 128:(t + 1) * 128], kdst=KT[grp][:, :, t * 128:(t + 1) * 128],
                                 vx=VX[grp][:, t, :, :], b_vx=b_VX[grp][t], kout=nk[r:r + 128, :], vout=nv[r:r + 128, :],
                                 grp=grp, t=t))

        def batch_att(grp, bi):
            r0 = grp * 512
            bo = bi * 256
            ajobs = []
            for h in range(4):
                kts = [(KT[grp][:, h, bo + kt * 128: bo + (kt + 1) * 128], b_KT, VX[grp][:, bi * 2 + kt, h, 0:129],
                        b_VX[grp][bi * 2 + kt]) for kt in range(2)]
                ajobs.append(dict(q=QT[grp][:, h, bo:bo + 256], kts=kts, nq=256, col0=r0 + bo, h=h))
            for _ in attention_multi(T, ajobs):
                yield

        def group_dft(grp):
            r0 = grp * 512
            uv_tiles(FT[grp], b_FT, 0, 4, UV, b_UV, 0)
            yield
            for bi in range(2):
                bo = bi * 256
                if bi == 1:
                    yield
                for g in range(4):
                    pf, pb, bb, _bi = nb()
                    i = 0
                    for nt in range(2):
                        for cs in range(2):
                            add("pe", lambda e, g=g, nt=nt, cs=cs, pf=pf, i=i, bi=bi: e.matmul(
                                pf[:, 0:256], lhsT=UV[:, bi * 2 + nt, g, cs * 128:(cs + 1) * 128], rhs=dftp[:, nt, cs, :],
                                start=(i == 0), stop=(i == 3)), reads=[b_UV, b_dftp] + RC, writes=[bb])
                            i += 1
                    col = r0 + bo
                    if g % 2 == 0:
                        add("dve", lambda e, g=g, pf=pf, col=col: e.tensor_copy(out=CAT[:, 4 + g, col:col + 256], in_=pf[:, 0:256]),
                            reads=[], writes=[bb, T.b_cat])
                    else:
                        add("act", lambda e, g=g, pf=pf, col=col: e.copy(out=CAT[:, 4 + g, col:col + 256], in_=pf[:, 0:256]),
                            reads=[], writes=[bb, T.b_cat])

        def tile_end(J):
            out = []
            if J["t"] in (1, 3):
                out.append((False, batch_att(J["grp"], J["t"] // 2)))
            if J["t"] == 3:
                out.append((True, group_dft(J["grp"])))
            return out

        if os.environ.get('MK_DBG1'):
            while late_mod:
                late_mod.pop(0)()
        S.mark('fp_front')
        front_pipeline(P, Q, W, jobs, lambda grp: (FT[grp], 0), b_FT, T.b_q, b_KT, group_end=tile_end, extra_steps=late_mod, bg_rate=int(os.environ.get('MK_BGRATE', '1')), after_loop=after_loop)
        while late_mod:
            late_mod.pop(0)()
        S.fence()
        A.off = mod_mark

    def front_sample(before_last_dft=None):
        mark = A.off
        QT = A.alloc([128, 4, 1024], BF16)
        KT = A.alloc([128, 4, 2560], BF16)
        VX = A.alloc([128, 20, 4, 130], BF16)
        FTA = A.alloc([128, 4, 2048], BF16)
        b_KT, b_FT = S.buf("KTs"), S.buf("FTs")
        b_VX = S.bufs("VXs", 20)
        W = Wt()
        win_off = A.off
        W.WIN = A.alloc([128, 8, 2048], BF16)
        assert win_off + (0 if win_off % 64 == 0 else 64 - win_off % 64) == WIN_S_OFF, (win_off, WIN_S_OFF)
        if "WIN" in PRE_S:
            W.b_win = PRE_S["b_win"]
            W.d_win = PRE_S["d_win"]
        else:
            W.b_win = S.bufs("win", 8)
            W.d_win = [S.dsem(f"wins{k}") for k in range(8)]
            load_win(W)
        pipe_off = A.off
        P = make_pipe(2, True)
        Q = make_qk()
        b_q = S.buf("qTs")
        for t0_ in range(0, 20, 4):
            add("pool", lambda e, t0_=t0_: e.memset(VX[:, t0_:t0_ + 4, :, 128:130], 1.0), reads=[], writes=b_VX[t0_:t0_ + 4])
        CKB = A.alloc([128, 4, 512], BF16)
        b_ckb = S.buf("ckb")
        d_ck = S.dsem("ck")
        d_cv = S.dsem("cv")
        S.add("pool", lambda e: e.dma_start(out=CKB, in_=ck.rearrange("(t p) n -> p t n", p=128)), writes=[b_ckb], dsem=d_ck)
        cvops = []
        for t in range(4):
            cvops.append(S.add("pool", lambda e, t=t: e.dma_start(out=VX[:, t, :, 0:128],
                                                                  in_=cv[t * 128:(t + 1) * 128, :].rearrange("p (h e) -> p h e", e=128)),
                               writes=[b_VX[t]], dsem=d_cv))
        for o in cvops:
            o.dcount = d_cv.count
        def ck_step(t):
            def run():
                tf, tb, tbb, _bi = nb()
                for hh in range(4):
                    add("pe", lambda e, hh=hh, t=t, tb=tb: e.transpose(out=tb[:, hh * 128:(hh + 1) * 128],
                                                                       in_=CKB[:, t, hh * 128:(hh + 1) * 128], identity=ident),
                        reads=[b_ckb] + RC, writes=[tbb])
                add("act", lambda e, t=t, tb=tb: e.copy(out=KT[:, :, t * 128:(t + 1) * 128],
                                                        in_=tb[:, 0:512].rearrange("p (a b) -> p a b", b=128)),
                    reads=[], writes=[tbb, b_KT])
            return run
        ck_steps = [lambda: None, lambda: None] + [ck_step(t) for t in range(4)]
        jobs = []
        for grp in range(4):
            own = grp < 2
            for t in range(4):
                r = grp * 512 + t * 128
                jobs.append(dict(x=xs[r:r + 128, :], v=1, need_q=own, rope=grp * 4 + t,
                                 qdst=QT[:, :, r:r + 128] if own else None, kdst=KT[:, :, 512 + r: 512 + r + 128],
                                 vx=VX[:, 4 + grp * 4 + t, :, :], b_vx=b_VX[4 + grp * 4 + t], kout=None, vout=None,
                                 grp=grp, t=t))
        S.mark('fs_front')
        front_pipeline(P, Q, W, jobs, lambda grp: (FTA, grp * 512), b_FT, b_q, b_KT, extra_steps=ck_steps)
        while ck_steps:
            ck_steps.pop(0)()
        S.fence()
        A.off = win_off
        UV = A.alloc([128, 16, 4, 256], BF16)
        b_UV = S.buf("UVs")
        assert A.off <= pipe_off
        A.off = pipe_off
        T = make_att(6, 8, 8)
        T.b_q = b_q
        NDS = 8
        DS = [A.alloc([128, 2, 512], BF16) for _ in range(NDS)]
        b_ds = S.bufs("ds", NDS)
        d_ds = [S.dsem(f"ds{i}") for i in range(NDS)]
        S.mark('fs_uv')
        uv_tiles(FTA, b_FT, 0, 16, UV, b_UV, 0)
        dsc = [0]

        def dft_chunk(nc_):
            accb = [nb() for _ in range(4)]
            for nt in range(16):
                s = dsc[0] % NDS
                dsc[0] += 1
                S.add("sp", lambda e, s=s, nt=nt: e.dma_start(
                    out=DS[s], in_=dfts_d[nt * 128:(nt + 1) * 128, :, nc_ * 512:(nc_ + 1) * 512]),
                    writes=[b_ds[s]], dsem=d_ds[s])
                for g in range(4):
                    for cs in range(2):
                        add("pe", lambda e, s=s, nt=nt, g=g, cs=cs: e.matmul(
                            accb[g][0], lhsT=UV[:, nt, g, cs * 128:(cs + 1) * 128], rhs=DS[s][:, cs, :],
                            start=(nt == 0 and cs == 0), stop=(nt == 15 and cs == 1)), reads=[b_UV, b_ds[s]], writes=[accb[g][2]])
            for g in range(4):
                pf = accb[g][0]
                col = nc_ * 512
                if g % 2 == 0:
                    add("dve", lambda e, g=g, pf=pf, col=col: e.tensor_copy(out=CAT[:, 4 + g, col:col + 512], in_=pf),
                        reads=[], writes=[accb[g][2], T.b_cat])
                else:
                    add("act", lambda e, g=g, pf=pf, col=col: e.copy(out=CAT[:, 4 + g, col:col + 512], in_=pf),
                        reads=[], writes=[accb[g][2], T.b_cat])

        for qc in range(2):
            S.mark(f'fs_att{qc}')
            ajobs = []
            for h in range(4):
                kts = [(KT[:, h, kt * 128:(kt + 1) * 128], b_KT, VX[:, kt, h, 0:129], b_VX[kt]) for kt in range(20)]
                ajobs.append(dict(q=QT[:, h, qc * 512:(qc + 1) * 512], kts=kts, nq=512, col0=qc * 512, h=h))
            for _ in attention_multi(T, ajobs):
                pass
            S.mark(f'fs_dft{qc}')
            if qc == 1 and before_last_dft is not None:
                before_last_dft()
            dft_chunk(qc)
        S.fence()
        A.off = mark

    phases = debug_phases or ("fp", "bp", "fs", "bs")
    BP = [None]
    if "fp" in phases:
        if "bp" in phases:
            def hook():
                BP[0] = back_phase(xp, 0, yp, "p", prefetch=True, win_prefetch=("fs" in phases))
                next(BP[0])
            front_prompt(hook)
        else:
            front_prompt()
    if "bp" in phases:
        if BP[0] is None:
            BP[0] = back_phase(xp, 0, yp, "p", prefetch=False, win_prefetch=("fs" in phases))
        for _ in BP[0]:
            pass
    BS = [None]
    if "fs" in phases:
        if "bs" in phases:
            def hook2():
                BS[0] = back_phase(xs, 1, ys, "s", prefetch=True, pf_w1=False)
                next(BS[0])
            front_sample(hook2)
        else:
            front_sample()
    if "bs" in phases:
        if BS[0] is None:
            BS[0] = back_phase(xs, 1, ys, "s")
        for _ in BS[0]:
            pass
    S.mark('end')
    S.emit()
    import json as _json
    if os.environ.get('MK_MARKS'):
        _json.dump(S.marks, open(os.environ['MK_MARKS'], 'w'))
        _json.dump(S.waitlog, open(os.environ['MK_MARKS'] + '.waits', 'w'))
    print("arena peak bytes", A.peak, "ops", len(S.ops), {e: len(S.eng_ops[e]) for e in QUEUES})
    return nc


def _consts(half):
    bf = ml_dtypes.bfloat16
    ident = np.eye(128, dtype=np.float32).astype(bf)
    c = np.arange(128)
    ang = 2 * np.pi * ((c[:, None] * c[None, :]) % 128) / 128.0
    ccs = np.concatenate([np.cos(ang), -np.sin(ang)], axis=1).astype(np.float32).astype(bf)
    n = np.arange(256)
    ang = 2 * np.pi * ((n[:, None] * n[None, :]) % 256) / 256.0
    sc = 1.0 / np.sqrt(256.0 * 128.0)
    dftp = np.stack([np.cos(ang) * sc, np.sin(ang) * sc], axis=1).astype(np.float32).astype(bf)
    pos = np.concatenate([half * 1024 + np.arange(1024), (1 - half) * 1024 + np.arange(1024)])
    npos = half * 1024 + np.arange(1024)
    ang = 2 * np.pi * ((pos[:, None].astype(np.int64) * npos[None, :]) % 2048) / 2048.0
    sc = 1.0 / np.sqrt(2048.0 * 128.0)
    dfts = np.stack([np.cos(ang) * sc, np.sin(ang) * sc], axis=1).astype(np.float32).astype(bf)
    inv = (np.float32(10000.0) ** (-np.arange(0, 32, 2, dtype=np.float32) / np.float32(32))).astype(np.float32)
    row = (pos // 64).astype(np.float32)
    col = (pos % 64).astype(np.float32)
    ar = row[:, None] * inv[None, :]
    ac = col[:, None] * inv[None, :]
    cos = np.concatenate([np.cos(ar), np.cos(ac)], axis=1)
    sin = np.concatenate([np.sin(ar), np.sin(ac)], axis=1)
    rope = np.stack([cos, sin], axis=1).astype(np.float32)
    dftp = np.ascontiguousarray(dftp.reshape(2, 128, 2, 256).transpose(1, 0, 2, 3).reshape(128, 1024))
    rope = np.ascontiguousarray(rope.reshape(16, 128, 2, 32).transpose(1, 0, 2, 3).reshape(128, 1024))
    return dict(ident=ident, ccs=ccs, dftp=dftp, dfts=dfts, rope=rope, ident32=np.eye(64, dtype=np.float32))


_NC_CACHE = {}


def kernel(x_prompt, x_sample, c, cache_k, cache_v, c_ctx, w_mod, b_mod, norm1_g, w_in,
           q_norm_g, k_norm_g, lambda_q1, lambda_k1, lambda_q2, lambda_k2, subln_g,
           w_four, w_out, norm2_g, w1, w2):
    f = lambda a: np.ascontiguousarray(np.asarray(a, dtype=np.float32))
    x_prompt, x_sample, c, cache_k, cache_v, c_ctx = map(f, (x_prompt, x_sample, c, cache_k, cache_v, c_ctx))
    phases = os.environ.get("MK_PHASES")
    phases = tuple(phases.split(",")) if phases else None
    key = phases
    if key not in _NC_CACHE:
        _NC_CACHE[key] = build_nc(phases)
    nc = _NC_CACHE[key]
    shared = dict(
        w_mod=f(w_mod)[0], b_mod=f(b_mod)[0], norm1_g=f(norm1_g)[0], w_in=f(w_in)[0], qg=f(q_norm_g)[0], kg=f(k_norm_g)[0],
        lam4=np.concatenate([f(lambda_q1)[0], f(lambda_k1)[0], f(lambda_q2)[0], f(lambda_k2)[0]]),
        subg=f(subln_g)[0], w_four=f(w_four)[0], w_out=f(w_out)[0], norm2_g=f(norm2_g)[0], w1=f(w1)[0], w2=f(w2)[0])
    cons = [_consts(0), _consts(1)]
    xpf = x_prompt.reshape(8192, 1024)
    in_maps = []
    for i in range(8):
        b, half = i // 2, i % 2
        own = x_sample[b, half * 1024:(half + 1) * 1024]
        oth = x_sample[b, (1 - half) * 1024:(2 - half) * 1024]
        m = dict(shared)
        m.update(cons[half])
        m.update(
            xp=np.ascontiguousarray(xpf[i * 1024:(i + 1) * 1024]),
            xs=np.ascontiguousarray(np.concatenate([own, oth], axis=0)),
            cvec=np.ascontiguousarray(np.stack([c_ctx, c[b]], axis=0)),
            ck=np.ascontiguousarray(cache_k[b, 0].reshape(512, 512)),
            cv=np.ascontiguousarray(cache_v[b, 0].reshape(512, 512)),
        )
        in_maps.append(m)
    res = run_bass_kernel_spmd(nc, in_maps, core_ids=list(range(8)))
    R = res.results
    y_prompt = np.concatenate([R[i]["yp"] for i in range(8)], axis=0).reshape(32, 256, 1024)
    y_sample = np.concatenate([R[i]["ys"] for i in range(8)], axis=0).reshape(4, 2048, 1024)
    nkk = np.concatenate([R[i]["nk"] for i in range(8)], axis=0).reshape(32, 1, 256, 4, 2, 64)
    nvv = np.concatenate([R[i]["nv"] for i in range(8)], axis=0).reshape(32, 1, 256, 4, 128)
    return (y_prompt.astype(np.float32), y_sample.astype(np.float32), nkk.astype(np.float32), nvv.astype(np.float32))
```
